# Optimizing a Trainium2 kernel written in Bass

```python
import jax, jax.numpy as jnp
from jax import lax
import numpy as np

D_MODEL = 1024
BATCH = 2
SEQ = 8192
DEPTH = 2

CONV_CH = D_MODEL // 2
CONV_WIDTH = 31
N_HEADS = 8
HEAD_DIM = 64
ATTN_WIDTH = N_HEADS * HEAD_DIM
D_FF = ((8 * D_MODEL // 3 + 255) // 256) * 256
Q_BLOCK = 128
EPS = 1e-6

COL_SIZES = (2 * CONV_CH,
             ATTN_WIDTH,
             ATTN_WIDTH,
             ATTN_WIDTH,
             N_HEADS,
             D_MODEL,
             D_MODEL)
IN_COLS = sum(COL_SIZES)
COL_SPLITS = tuple(int(s) for s in np.cumsum(COL_SIZES)[:-1])

kernel_name = "hybrid_conformer_conv_fox_attention_swiglu"


def rmsnorm(x, g):
    xf = x.astype(jnp.float32)
    inv = lax.rsqrt(jnp.mean(xf * xf, axis=-1, keepdims=True) + EPS)
    return (xf * inv).astype(x.dtype) * g


def layernorm(x, g, b):
    xf = x.astype(jnp.float32)
    mu = jnp.mean(xf, axis=-1, keepdims=True)
    var = jnp.mean(jnp.square(xf - mu), axis=-1, keepdims=True)
    return ((xf - mu) * lax.rsqrt(var + EPS)).astype(x.dtype) * g + b


def causal_depthwise_conv(a, w, b):
    out = lax.conv_general_dilated(
        a, w[:, None, :], window_strides=(1,), padding=[(CONV_WIDTH - 1, 0)],
        dimension_numbers=("NWC", "WIO", "NWC"), feature_group_count=CONV_CH)
    return out + b


def conformer_conv_branch(a_in, w_dw, b_dw, g_ln, b_ln, w_out):
    a = a_in[..., :CONV_CH] * jax.nn.sigmoid(a_in[..., CONV_CH:])
    a = causal_depthwise_conv(a, w_dw, b_dw)
    a = jax.nn.silu(layernorm(a, g_ln, b_ln))
    return a @ w_out


def forgetting_attention(q, k, v, f_logit, b_forget, g_q, g_k):
    B, S, _ = q.shape
    nb = S // Q_BLOCK
    def heads(t, g):
        t = t.reshape(B, S, N_HEADS, HEAD_DIM)
        if g is not None:
            t = rmsnorm(t, g)
        return t.transpose(0, 2, 1, 3)
    qh, kh, vh = heads(q, g_q), heads(k, g_k), heads(v, None)
    log_f = jax.nn.log_sigmoid(f_logit.astype(jnp.float32) + b_forget.astype(jnp.float32))
    c = jnp.cumsum(log_f, axis=1).transpose(0, 2, 1)
    scale = HEAD_DIM ** -0.5
    q_blocks = qh.reshape(B, N_HEADS, nb, Q_BLOCK, HEAD_DIM).transpose(2, 0, 1, 3, 4)
    c_blocks = c.reshape(B, N_HEADS, nb, Q_BLOCK).transpose(2, 0, 1, 3)
    k_pos = jnp.arange(S)

    def one_block(args):
        qi, ci, i = args
        s = jnp.einsum("bhqd,bhkd->bhqk", qi, kh).astype(jnp.float32) * scale
        s = s + ci[..., :, None] - c[:, :, None, :]
        q_pos = i * Q_BLOCK + jnp.arange(Q_BLOCK)
        s = jnp.where(k_pos[None, :] <= q_pos[:, None], s, -jnp.inf)
        p = jax.nn.softmax(s, axis=-1).astype(vh.dtype)
        return jnp.einsum("bhqk,bhkd->bhqd", p, vh)

    out = lax.map(one_block, (q_blocks, c_blocks, jnp.arange(nb)))
    return out.transpose(1, 0, 3, 2, 4).reshape(B, S, ATTN_WIDTH)


def setup_inputs(seed: int = 0) -> dict:
    key = jax.random.key(seed)
    ks = jax.random.split(key, 20)
    f32 = jnp.float32
    def nrm(k, shape, fan_in):
        return jax.random.normal(k, shape, f32) * fan_in ** -0.5
    def gain(k, shape):
        return 1.0 + 0.02 * jax.random.normal(k, shape, f32)
    return {
        "x": jax.random.normal(ks[0], (BATCH, SEQ, D_MODEL), f32),
        "g_mix": gain(ks[1], (DEPTH, D_MODEL)),
        "w_in": nrm(ks[2], (DEPTH, D_MODEL, IN_COLS), D_MODEL),
        "b_forget": jax.random.uniform(ks[3], (DEPTH, N_HEADS), f32, 2.0, 5.0),
        "w_dw": nrm(ks[4], (DEPTH, CONV_WIDTH, CONV_CH), CONV_WIDTH),
        "b_dw": 0.02 * jax.random.normal(ks[5], (DEPTH, CONV_CH), f32),
        "g_conv_ln": gain(ks[6], (DEPTH, CONV_CH)),
        "b_conv_ln": 0.02 * jax.random.normal(ks[7], (DEPTH, CONV_CH), f32),
        "w_conv_out": nrm(ks[8], (DEPTH, CONV_CH, D_MODEL), CONV_CH),
        "g_q": gain(ks[9], (DEPTH, HEAD_DIM)),
        "g_k": gain(ks[10], (DEPTH, HEAD_DIM)),
        "w_attn_out": nrm(ks[11], (DEPTH, ATTN_WIDTH, D_MODEL), ATTN_WIDTH),
        "w_out": nrm(ks[12], (DEPTH, D_MODEL, D_MODEL), D_MODEL),
        "g_ffn": gain(ks[13], (DEPTH, D_MODEL)),
        "w_ffn_in": nrm(ks[14], (DEPTH, D_MODEL, 2 * D_FF), D_MODEL),
        "w_ffn_out": nrm(ks[15], (DEPTH, D_FF, D_MODEL), D_FF),
    }


def reference(x, g_mix, w_in, b_forget, w_dw, b_dw, g_conv_ln, b_conv_ln, w_conv_out,
              g_q, g_k, w_attn_out, w_out, g_ffn, w_ffn_in, w_ffn_out):
    for l in range(DEPTH):
        h = rmsnorm(x, g_mix[l])
        proj = h @ w_in[l]
        a_in, q, k, v, f_logit, gate_a, gate_b = jnp.split(proj, COL_SPLITS, axis=-1)
        y_conv = conformer_conv_branch(a_in, w_dw[l], b_dw[l], g_conv_ln[l],
                                       b_conv_ln[l], w_conv_out[l])
        y_attn = forgetting_attention(q, k, v, f_logit, b_forget[l], g_q[l], g_k[l]) @ w_attn_out[l]
        merged = jax.nn.sigmoid(gate_a) * y_conv + jax.nn.sigmoid(gate_b) * y_attn
        x = x + merged @ w_out[l]
        h2 = rmsnorm(x, g_ffn[l])
        gu = h2 @ w_ffn_in[l]
        x = x + (jax.nn.silu(gu[..., :D_FF]) * gu[..., D_FF:]) @ w_ffn_out[l]
    return x
```

```python
import contextlib
import numpy as np
import concourse.bass as bass
import concourse.mybir as mybir
from concourse.bass_utils import run_bass_kernel_spmd

F32 = mybir.dt.float32
BF16 = mybir.dt.bfloat16
AF = mybir.ActivationFunctionType
ALU = mybir.AluOpType
ENG = ["sp", "act", "dve", "pool", "pe"]
SAME_ENG_SYNC = True

D = 1024
S = 8192
TOK = 2048
NT = 16
DFF = 2816
NFC = 22
INC = 4616
EPS = 1e-6
GROUPS = [[0, 1, 2, 3], [4, 5, 6, 7]]
import os
AG_QOS = os.environ.get("K_AG_QOS", "P3")


class T:
    psum = False

    def __init__(self, name, h, sem=None):
        self.name = name
        self.sem = sem or name
        self.h = h
        self.w = {}
        self.r = {}

    def __getitem__(self, k):
        return self.h[k]


class Sched:
    def __init__(self, nc, stack):
        self.nc = nc
        self.stack = stack
        self.ops = {e: [] for e in ENG}
        self.base = {e: 0 for e in ENG}
        self.esem = {e: stack.enter_context(nc.semaphore("es_" + e)) for e in ENG}
        self.dsem = {}
        self.waited = {e: {} for e in ENG}
        self.bar = stack.enter_context(nc.semaphore("bar"))
        self.nbar = 0
        self.tiles = []
        self.pid = {}

    def track(self, t):
        self.tiles.append(t)
        return t

    def _dsem(self, name):
        if name not in self.dsem:
            self.dsem[name] = [self.stack.enter_context(self.nc.semaphore("ds_" + name)), 0]
        return self.dsem[name]

    def _collect(self, eng, reads, writes, pwrites):
        waits = {}

        def add(d):
            for k, v in d.items():
                if k[0] == "c" and k[1] == eng and (eng == "pe" or not SAME_ENG_SYNC):
                    continue
                if waits.get(k, -1) < v:
                    waits[k] = v

        for t in reads:
            add(t.w)
        for t in writes:
            add(t.w)
            add(t.r)
        for t in pwrites:
            add(t.r)
        return waits

    def _mark(self, waits):
        for k, v in waits.items():
            if k[0] == "c":
                self.ops[k[1]][v]["signal"] = True

    def op(self, eng, fn, reads=(), writes=(), pwrites=()):
        pr = [t for t in reads if t.psum]
        if pr:
            reads = [t for t in reads if not t.psum]
            writes = list(writes) + [t for t in pr if t not in writes]
        waits = self._collect(eng, reads, writes, pwrites)
        self._mark(waits)
        idx = len(self.ops[eng])
        self.ops[eng].append(dict(fn=fn, waits=waits, signal=False, dma=None))
        key = ("c", eng)
        for t in reads:
            t.r[key] = idx
        for t in writes:
            t.w = {key: idx}
            t.r = {}
        for t in pwrites:
            t.w[key] = idx

    def dma(self, q, fn, semtile, reads=(), writes=(), pwrites=(), inc=16):
        waits = self._collect(q, reads, writes, pwrites)
        self._mark(waits)
        ds = self._dsem(semtile.sem)
        ds[1] += inc
        val = ds[1]
        self.ops[q].append(dict(fn=fn, waits=waits, signal=False, dma=(ds[0], inc)))
        key = ("d", semtile.sem)
        for t in reads:
            t.r[key] = val
        for t in writes:
            t.w = {key: val}
            t.r = {}
        for t in pwrites:
            t.w[key] = val

    def emit(self, scratch_a, scratch_b, keep=()):
        nc = self.nc
        prefix = {}
        for e in ENG:
            c = self.base[e]
            pf = []
            for o in self.ops[e]:
                if o["signal"]:
                    c += 1
                pf.append(c)
            prefix[e] = pf
        self.nbar += 1
        nbar = self.nbar
        keep_sems = set(t.sem for t in keep)
        dsem_final = {k: (v[0], v[1]) for k, v in self.dsem.items() if k not in keep_sems}
        last_cnt = {e: (prefix[e][-1] if prefix[e] else self.base[e]) for e in ENG}

        def run(e, h):
            wd = self.waited[e]

            def wait(sem, key, val):
                if wd.get(key, 0) >= val:
                    return
                wd[key] = val
                h.wait_ge(sem, val)

            for o in self.ops[e]:
                for k, v in o["waits"].items():
                    if k[0] == "c":
                        wait(self.esem[k[1]], k, prefix[k[1]][v])
                    else:
                        wait(self.dsem[k[1]][0], k, v)
                ins = o["fn"](h)
                if o["dma"] is not None:
                    ins.then_inc(o["dma"][0], o["dma"][1])
                elif o["signal"]:
                    ins.then_inc(self.esem[e], 1)
            if e == "sp":
                for k, (sem, val) in dsem_final.items():
                    if val > 0:
                        wait(sem, ("d", k), val)
                for e2 in ENG:
                    if e2 != "sp" and last_cnt[e2] > 0:
                        wait(self.esem[e2], ("c", e2), last_cnt[e2])
                h.dma_start(out=scratch_b[:, :], in_=scratch_a[:, :]).then_inc(self.bar, 16)
            h.wait_ge(self.bar, 16 * nbar)

        for e in ENG:
            if e != "sp" and self.ops[e]:
                for o in reversed(self.ops[e]):
                    if o["dma"] is None:
                        if not o["signal"]:
                            o["signal"] = True
                        break
        for e in ENG:
            c = self.base[e]
            pf = []
            for o in self.ops[e]:
                if o["signal"]:
                    c += 1
                pf.append(c)
            prefix[e] = pf
            last_cnt[e] = pf[-1] if pf else self.base[e]

        with nc.Block() as block:
            @block.sync
            def _(h):
                run("sp", h)

            @block.scalar
            def _(h):
                run("act", h)

            @block.vector
            def _(h):
                run("dve", h)

            @block.gpsimd
            def _(h):
                run("pool", h)

            @block.tensor
            def _(h):
                run("pe", h)

        for e in ENG:
            self.base[e] = last_cnt[e]
            self.ops[e] = []
        for t in self.tiles:
            if t in keep:
                t.w = {k: v for k, v in t.w.items() if k[0] == "d"}
                t.r = {}
                continue
            t.w = {}
            t.r = {}


def build(n_layers=2, debug=False, stop_phase=None):
    nc = bass.Bass("TRN2", target_bir_lowering=False)
    stack = contextlib.ExitStack()
    with stack:
        sc = Sched(nc, stack)

        def dram(name, shape, dt, kind=None):
            if kind is None:
                h = nc.dram_tensor(name, shape, dt)
            else:
                h = nc.dram_tensor(name, shape, dt, kind=kind)
            return sc.track(T(name, h.ap()))

        def dbgdram(name, shape, dt):
            return dram(name, shape, dt)

        dbg_done = set()

        def dump(src, shape, dt):
            if not debug or src.name in dbg_done:
                return
            dbg_done.add(src.name)
            dst = dram("d_" + src.name, shape, dt, "ExternalOutput")
            dst.sem = "dbg"
            sc.dma("sp", lambda h: h.dma_start(out=dst.h, in_=src.h), dst, reads=[src], writes=[dst])

        x_in = dram("x", [TOK, D], F32, "ExternalInput")
        g_mix = dram("g_mix", [2, D], F32, "ExternalInput")
        w_in = dram("w_in", [2, D, INC], F32, "ExternalInput")
        b_forget = dram("b_forget", [2, 8], F32, "ExternalInput")
        w_dw = dram("w_dw", [2, 31, 512], F32, "ExternalInput")
        b_dw = dram("b_dw", [2, 512], F32, "ExternalInput")
        g_cln = dram("g_conv_ln", [2, 512], F32, "ExternalInput")
        b_cln = dram("b_conv_ln", [2, 512], F32, "ExternalInput")
        w_co = dram("w_conv_out", [2, 512, D], F32, "ExternalInput")
        g_q = dram("g_q", [2, 64], F32, "ExternalInput")
        g_k = dram("g_k", [2, 64], F32, "ExternalInput")
        w_ao = dram("w_attn_out", [2, 512, D], F32, "ExternalInput")
        w_out = dram("w_out", [2, D, D], F32, "ExternalInput")
        g_ffn = dram("g_ffn", [2, D], F32, "ExternalInput")
        w_f1 = dram("w_ffn_in", [2, D, 2 * DFF], F32, "ExternalInput")
        w_f2 = dram("w_ffn_out", [2, DFF, D], F32, "ExternalInput")
        consts = dram("consts", [128, 256], F32, "ExternalInput")
        cmask = dram("cmask", [128, 1], F32, "ExternalInput")
        y_out = dram("y", [TOK, D], F32, "ExternalOutput")

        qk_in = dbgdram("qk_in", [1024, TOK], BF16)
        qk_ag = dram("qk_ag", [4096 + 256, TOK], BF16)
        v_in = dbgdram("v_in", [2 * TOK, 256], BF16)
        v_ag = dram("v_ag", [2 * S, 256], BF16)
        lf_in = dbgdram("lf_in", [TOK, 8], F32)
        lf_ag = dram("lf_ag", [S, 8], F32)
        tl_in = dram("tl_in", [512, 32], BF16)
        tl_ag = dram("tl_ag", [2048, 32], BF16)
        sg_sc = dram("sg_sc", [2048, TOK], BF16)
        o_in = dbgdram("o_in", [256, S // 2], BF16)
        o_ag = dram("o_ag", [1024, S // 2], BF16)
        x1_sc = dbgdram("x1_sc", [TOK, D], F32)
        xr_sc = dram("xr_sc", [TOK, D], F32)
        cv_dbg = dram("cv_dbg", [512, TOK], BF16)
        bsa = dram("bsa", [1, 16], F32)
        bsb = dram("bsb", [1, 16], F32)

        uniq = [0]

        def sb(st, name, shape, dt):
            uniq[0] += 1
            return sc.track(T(name, st.enter_context(nc.sbuf_tensor("%s_%d" % (name, uniq[0]), shape, dt)), sem=name))

        def ps(st, name, shape, dt):
            t_ = sc.track(T(name, st.enter_context(nc.psum_tensor(name, shape, dt))))
            t_.psum = True
            return t_

        psn = [0]

        def alloc_psum(st, nf, nt):
            psn[0] += 1
            a = [ps(st, "ps%d_%d" % (psn[0], i), [128, 512], F32) for i in range(nf)]
            b = [ps(st, "pst%d_%d" % (psn[0], i), [128, 8, 128], BF16) for i in range(nt)]
            return a, b

        PS, PSTs = [], []
        ident = sb(stack, "ident", [128, 128], BF16)
        tri = sb(stack, "tri", [128, 128], BF16)
        trif = sb(stack, "trif", [128, 128], F32)
        onesf = sb(stack, "onesf", [128, 128], F32)
        bones = sb(stack, "bones", [128, 128], BF16)
        ones512 = sb(stack, "ones512", [128, 128], BF16)
        cm = sb(stack, "cm", [128, 1], F32)
        hTs = [sb(stack, "hT%d" % i, [128, 8, 512], BF16) for i in range(4)]

        bsa.sem = "dbg"
        sc.dma("sp", lambda h: h.dma_start(out=bsa[:, :], in_=consts[0:1, 0:16]), bsa, reads=[consts], writes=[bsa])
        sc.dma("pool", lambda h: h.dma_start(out=ident[:], in_=consts[:, 0:128]), ident, reads=[consts], writes=[ident])
        sc.dma("pool", lambda h: h.dma_start(out=tri[:], in_=consts[:, 128:256]), tri, reads=[consts], writes=[tri])
        sc.dma("sp", lambda h: h.dma_start(out=trif[:], in_=consts[:, 128:256]), trif, reads=[consts], writes=[trif])
        sc.dma("sp", lambda h: h.dma_start(out=cm[:], in_=cmask[:, :]), cm, reads=[cmask], writes=[cm])
        sc.op("dve", lambda h: h.memset(onesf[:], 1.0), writes=[onesf])
        sc.op("dve", lambda h: h.memset(bones[:], 0.0), writes=[bones])
        sc.op("dve", lambda h: h.memset(bones[0:64, 0:64], 1.0), pwrites=[bones], reads=[bones])
        sc.op("dve", lambda h: h.memset(bones[64:128, 64:128], 1.0), pwrites=[bones], reads=[bones])
        sc.op("dve", lambda h: h.memset(ones512[:], 1.0 / 512.0), writes=[ones512])

        if stop_phase is not None:
            jd = dram("junkd", [20, 8], F32)
            jd.sem = "dbg"
            for ii, tt_ in enumerate([g_mix, w_in, b_forget, w_dw, b_dw, g_cln, b_cln, w_co, g_q, g_k, w_ao, w_out, g_ffn, w_f1, w_f2]):
                flat = tt_.h
                while len(flat.shape) > 1:
                    flat = flat[0]
                sc.dma("sp", lambda h, ii=ii, flat=flat: h.dma_start(out=jd[ii:ii + 1, :], in_=flat[0:8].unsqueeze(0)), jd, reads=[tt_], pwrites=[jd])

        dyn_cache = {}
        DYN = {
            "halo": (lambda j: ((j + 3) % 4) * 512, 1536),
            "qk": (lambda j: (j // 2) * 1024 + (j % 2) * 128, 1152),
            "lf": (lambda j: j * 2, 6),
            "va": (lambda j: (j // 2) * 8192, 8192),
            "vb": (lambda j: (j % 2) * 128, 128),
            "oa": (lambda j: (j // 2) * 512, 512),
            "ob": (lambda j: (j % 2) * 2048, 2048),
        }

        def dynv(h, eng, key):
            if (eng, key) not in dyn_cache:
                j = h.partition_id() % 4
                fn, mx = DYN[key]
                dyn_cache[(eng, key)] = h.snap(fn(j), min_val=0, max_val=mx)
            return dyn_cache[(eng, key)]

        def norm_transpose(st, t, xt, gB, ssq, lnv, rstd, junk, hb):
            sc.op("act", lambda h: h.activation(out=junk[:], in_=xt[:], func=AF.Square, accum_out=ssq[:, t:t + 1]),
                  reads=[xt], writes=[junk], pwrites=[ssq])
            sc.op("act", lambda h: h.activation(out=lnv[:, t:t + 1], in_=ssq[:, t:t + 1], func=AF.Ln, scale=1.0 / D, bias=EPS),
                  reads=[ssq], pwrites=[lnv])
            sc.op("act", lambda h: h.activation(out=rstd[:, t:t + 1], in_=lnv[:, t:t + 1], func=AF.Exp, scale=-0.5),
                  reads=[lnv], pwrites=[rstd])
            sc.op("dve", lambda h: h.scalar_tensor_tensor(out=hb[:], in0=xt[:], scalar=rstd[:, t:t + 1], in1=gB[:],
                                                          op0=ALU.mult, op1=ALU.mult),
                  reads=[xt, rstd, gB], writes=[hb])
            PST = PSTs[t % 2]
            for c in range(8):
                sc.op("pe", lambda h, c=c: h.transpose(out=PST[:, c, :], in_=hb[:, c * 128:(c + 1) * 128], identity=ident[:]),
                      reads=[hb, ident], writes=([PST] if c == 0 else []), pwrites=([PST] if c > 0 else []))
            if t % 2 == 0:
                sc.op("act", lambda h: h.activation(out=hTs[t // 4][:, :, (t % 4) * 128:(t % 4 + 1) * 128], in_=PST[:], func=AF.Copy),
                      reads=[PST], pwrites=[hTs[t // 4]])
            else:
                sc.op("dve", lambda h: h.tensor_copy(out=hTs[t // 4][:, :, (t % 4) * 128:(t % 4 + 1) * 128], in_=PST[:]),
                      reads=[PST], pwrites=[hTs[t // 4]])

        def wload(wt, src_ap_fn, srcT):
            sc.dma("pool", lambda h: h.dma_start(out=wt[:], in_=src_ap_fn()), wt, reads=[srcT], writes=[wt])

        if stop_phase == "0":
            sc.emit(bsa, bsb)
            n_layers = 0
        for l in range(n_layers):
            xcur = x_in if l == 0 else xr_sc
            xnext = y_out if l == n_layers - 1 else xr_sc
            with contextlib.ExitStack() as lst:
                cvT = sb(lst, "cvT", [128, 4, TOK], BF16)
                wdw = sb(lst, "wdw", [128, 4, 31], F32)
                bdw = sb(lst, "bdw", [128, 4], F32)
                gln = sb(lst, "gln", [128, 4], F32)
                bln = sb(lst, "bln", [128, 4], F32)
                Lt = sb(lst, "Lt", [128, 64, 2], F32)
                ctab = sb(lst, "ctab", [128, 64, 2], F32)
                Eb = [sb(lst, "Eb%d" % i, [128, 64, 2], F32) for i in range(2)]
                Tb = sb(lst, "Tb", [128, 64, 2], F32)
                wc = sb(lst, "wc", [128, 4, D], BF16)
                wa = sb(lst, "wa", [128, 4, D], BF16)
                wo = sb(lst, "wo", [128, 8, D], BF16)
                abscope = contextlib.ExitStack()
                aT = sb(abscope, "aT", [128, 4, 32 + TOK], BF16)
                dg = sb(abscope, "dg", [128, 4, 31, 128], BF16)
                with contextlib.ExitStack() as st:
                    PS, PSTs = alloc_psum(st, 6, 2)
                    xin = [sb(st, "xin%d" % i, [128, D], F32) for i in range(4)]
                    hb = [sb(st, "hb%d" % i, [128, D], BF16) for i in range(2)]
                    junk = sb(st, "junk", [128, D], BF16)
                    gB = sb(st, "gB", [128, D], F32)
                    ssq = sb(st, "ssq", [128, NT], F32)
                    lnv = sb(st, "lnv", [128, NT], F32)
                    rstd = sb(st, "rstd", [128, NT], F32)
                    wt = [sb(st, "wt%d" % i, [128, 8, 512], BF16) for i in range(4)]
                    gqk = sb(st, "gqk", [128, 2], F32)
                    bfB = sb(st, "bfB", [128, 8], F32)
                    sgt = [sb(st, "sgt%d" % i, [128, 512], F32) for i in range(2)]
                    sqq = [sb(st, "sqq%d" % i, [128, 512], BF16) for i in range(2)]
                    rq = [sb(st, "rq%d" % i, [128, 512], F32) for i in range(2)]
                    NOB = 6
                    ob = [sb(st, "ob%d" % i, [128, 512], BF16) for i in range(NOB)]
                    lft = sb(st, "lft", [128, NT, 8], F32)
                    lfe = sb(st, "lfe", [128, NT, 8], F32)

                    sc.dma("sp", lambda h: h.dma_start(out=xin[0][:], in_=xcur[0:128, :]), xin[0], reads=[xcur], writes=[xin[0]])
                    sc.dma("sp", lambda h: h.dma_start(out=gB[:], in_=g_mix[l:l + 1, :].partition_broadcast(128)), gB,
                           reads=[g_mix], writes=[gB])
                    sc.dma("act", lambda h: h.dma_start(out=bfB[:], in_=b_forget[l:l + 1, :].partition_broadcast(128)), bfB,
                           reads=[b_forget], writes=[bfB])
                    for hh in range(2):
                        sc.dma("act", lambda h, hh=hh: h.dma_start(out=gqk[hh * 64:(hh + 1) * 64, 0:1],
                                                                   in_=g_q[l, :].unsqueeze(1)), gqk, reads=[g_q], pwrites=[gqk])
                        sc.dma("act", lambda h, hh=hh: h.dma_start(out=gqk[hh * 64:(hh + 1) * 64, 1:2],
                                                                   in_=g_k[l, :].unsqueeze(1)), gqk, reads=[g_k], pwrites=[gqk])
                    sc.op("dve", lambda h: h.tensor_scalar(out=gqk[:, 0:1], in0=gqk[:, 0:1], scalar1=0.125, scalar2=None, op0=ALU.mult),
                          reads=[gqk], writes=[gqk])

                    colgroups = [("q", 1024, 512), ("k", 1536, 512), ("v", 2048, 512), ("f", 2560, 8), ("cv", 0, 512), ("cg", 512, 512),
                                 ("ga", 2568, 512), ("ga", 3080, 512), ("gb", 3592, 512), ("gb", 4104, 512)]
                    issued = [0]

                    def issue_w(gi):
                        kind, c0, n = colgroups[gi]
                        w = wt[gi % 4]
                        sc.dma("pool", lambda h: h.dma_start(out=w[:, :, 0:n],
                                                              in_=w_in[l][:, c0:c0 + n].rearrange("(c p) n -> p c n", p=128)),
                               w, reads=[w_in], writes=[w])

                    def prefetch(upto, busy=()):
                        while issued[0] <= min(upto, 9) and (issued[0] % 4) not in [b % 4 for b in busy]:
                            issue_w(issued[0])
                            issued[0] += 1

                    if stop_phase == "A00":
                        sc.emit(bsa, bsb)
                        break
                    prefetch(2)
                    for c4 in range(4):
                        sc.dma("pool", lambda h, c4=c4: h.dma_start(out=wdw[:, c4, :], in_=w_dw[l][:, c4 * 128:(c4 + 1) * 128].rearrange("k p -> p k"),
                                                                  allow_slow_non_contiguous=True), wdw, reads=[w_dw], pwrites=[wdw])
                    for (tt, src) in ((bdw, b_dw), (gln, g_cln), (bln, b_cln)):
                        sc.dma("pool", lambda h, tt=tt, src=src: h.dma_start(out=tt[:], in_=src[l].rearrange("(c p) -> p c", p=128),
                                                                          allow_slow_non_contiguous=True), tt, reads=[src], writes=[tt])
                    if stop_phase == "A01":
                        sc.emit(bsa, bsb)
                        break
                    sc.dma("sp", lambda h: h.dma_start(out=xin[1][:], in_=xcur[128:256, :]), xin[1], reads=[xcur], writes=[xin[1]])
                    sc.dma("sp", lambda h: h.dma_start(out=xin[2][:], in_=xcur[256:384, :]), xin[2], reads=[xcur], writes=[xin[2]])
                    for t in range(NT):
                        if t + 3 < NT:
                            sc.dma("sp", lambda h, t=t: h.dma_start(out=xin[(t + 3) % 4][:], in_=xcur[(t + 3) * 128:(t + 4) * 128, :]),
                                   xin[(t + 3) % 4], reads=[xcur], writes=[xin[(t + 3) % 4]])
                        norm_transpose(st, t, xin[t % 4], gB, ssq, lnv, rstd, junk, hb[t % 2])

                    if stop_phase == "A0":
                        sc.emit(bsa, bsb)
                        break

                    def mm_fm(pst, w, cc0, tg):
                        for c in range(8):
                            sc.op("pe", lambda h, c=c: h.matmul(pst[:], lhsT=w[:, c, cc0:cc0 + 128], rhs=hTs[tg][:, c, :],
                                                                 start=(c == 0), stop=(c == 7)),
                                  reads=[w, hTs[tg]], writes=([pst] if c == 0 else []), pwrites=([pst] if c > 0 else []))

                    def mm_tm(pst, w, n, t):
                        for c in range(8):
                            sc.op("pe", lambda h, c=c: h.matmul(pst[:, 0:n], lhsT=hTs[t // 4][:, c, (t % 4) * 128:(t % 4 + 1) * 128], rhs=w[:, c, 0:n],
                                                                 start=(c == 0), stop=(c == 7)),
                                  reads=[w, hTs[t // 4]], writes=([pst] if c == 0 else []), pwrites=([pst] if c > 0 else []))

                    cnt = 0
                    qk_groups = []
                    for qk in range(2):
                        for i in range(4):
                            for tg in range(4):
                                qk_groups.append((qk, i, tg))
                    ctx = {}

                    def qk_main(g):
                        qk, i, tg = qk_groups[g]
                        nonlocal_cnt = cnt + g
                        pq, p2 = PS[(nonlocal_cnt * 2) % 6], PS[(nonlocal_cnt * 2 + 1) % 6]
                        s_, r_, o_ = sqq[nonlocal_cnt % 2], rq[nonlocal_cnt % 2], ob[nonlocal_cnt % NOB]
                        ctx[g] = (pq, p2, s_, r_, o_)
                        if i == 0 and tg == 0:
                            prefetch(qk + 3, busy=[qk])
                        mm_fm(pq, wt[qk % 4], i * 128, tg)
                        sc.op("act", lambda h, pq=pq, s_=s_: h.activation(out=s_[:], in_=pq[:], func=AF.Square), reads=[pq], writes=[s_])

                    def qk_rest(g):
                        qk, i, tg = qk_groups[g]
                        pq, p2, s_, r_, o_ = ctx[g]
                        sc.op("pe", lambda h, p2=p2, s_=s_: h.matmul(p2[:], lhsT=bones[:], rhs=s_[:], start=True, stop=True),
                              reads=[bones, s_], writes=[p2])
                        sc.op("act", lambda h, p2=p2, r_=r_: h.activation(out=r_[:], in_=p2[:], func=AF.Ln, scale=1.0 / 64, bias=EPS),
                              reads=[p2], writes=[r_])
                        sc.op("act", lambda h, r_=r_: h.activation(out=r_[:], in_=r_[:], func=AF.Exp, scale=-0.5), reads=[r_], writes=[r_])
                        sc.op("dve", lambda h, pq=pq, r_=r_, o_=o_, qk=qk: h.scalar_tensor_tensor(
                            out=o_[:], in0=pq[:], scalar=gqk[:, qk:qk + 1], in1=r_[:], op0=ALU.mult, op1=ALU.mult),
                            reads=[pq, r_, gqk], writes=[o_])
                        r0 = qk * 512 + i * 128
                        sc.dma("sp", lambda h, o_=o_, r0=r0, tg=tg: h.dma_start(out=qk_in[r0:r0 + 128, tg * 512:(tg + 1) * 512], in_=o_[:]),
                               o_, reads=[o_], pwrites=[qk_in])

                    for g in range(len(qk_groups) + 1):
                        if g < len(qk_groups):
                            qk_main(g)
                        if g >= 1:
                            qk_rest(g - 1)
                    cnt += len(qk_groups)
                    for i in range(4):
                        for k in range(31):
                            sc.op("dve", lambda h, i=i, k=k: h.tensor_scalar(out=dg[:, i, k, :], in0=ident[:], scalar1=wdw[:, i, k:k + 1], scalar2=None, op0=ALU.mult),
                                  reads=[ident, wdw], pwrites=[dg])
                    prefetch(5, busy=[2])
                    for t in range(NT):
                        pv = PS[(cnt * 2) % 6]
                        o_ = ob[cnt % NOB]
                        cnt += 1
                        mm_tm(pv, wt[2], 512, t)
                        if t % 2 == 0:
                            sc.op("act", lambda h, pv=pv, o_=o_: h.activation(out=o_[:], in_=pv[:], func=AF.Copy), reads=[pv], writes=[o_])
                        else:
                            sc.op("dve", lambda h, pv=pv, o_=o_: h.tensor_copy(out=o_[:], in_=pv[:]), reads=[pv], writes=[o_])
                        for pi in range(2):
                            sc.dma("sp", lambda h, o_=o_, t=t, pi=pi: h.dma_start(out=v_in[pi * TOK + t * 128:pi * TOK + (t + 1) * 128, :],
                                                                                  in_=o_[:, pi * 256:(pi + 1) * 256]), o_,
                                   reads=[o_], pwrites=[v_in])
                    prefetch(6, busy=[3])
                    for t in range(NT):
                        pf = PS[(cnt * 2) % 6]
                        cnt += 1
                        mm_tm(pf, wt[3], 8, t)
                        sc.op("dve", lambda h, pf=pf, t=t: h.tensor_tensor(out=lft[:, t, :], in0=pf[:, 0:8], in1=bfB[:], op=ALU.add),
                              reads=[pf, bfB], pwrites=[lft])
                    sc.op("act", lambda h: h.activation(out=lfe[:], in_=lft[:], func=AF.Exp, scale=-1.0), reads=[lft], writes=[lfe])
                    sc.op("act", lambda h: h.activation(out=lft[:], in_=lfe[:], func=AF.Ln, bias=1.0), reads=[lfe], writes=[lft])
                    sc.op("dve", lambda h: h.tensor_scalar(out=lfe[:], in0=lft[:], scalar1=-1.0, scalar2=None, op0=ALU.mult), reads=[lft], writes=[lfe])
                    sc.dma("sp", lambda h: h.dma_start(out=lf_in.h.rearrange("(t p) e -> p t e", p=128), in_=lfe[:]), lfe, reads=[lfe], writes=[lf_in])
                    prefetch(7, busy=[4, 5])
                    for i in range(4):
                        for tg in range(4):
                            pv, pg = PS[(cnt * 2) % 6], PS[(cnt * 2 + 1) % 6]
                            s_ = sgt[cnt % 2]
                            cnt += 1
                            mm_fm(pg, wt[5 % 4], i * 128, tg)
                            mm_fm(pv, wt[4 % 4], i * 128, tg)
                            sc.op("act", lambda h, pg=pg, s_=s_: h.activation(out=s_[:], in_=pg[:], func=AF.Sigmoid), reads=[pg], writes=[s_])
                            sc.op("dve", lambda h, pv=pv, s_=s_, i=i, tg=tg: h.tensor_tensor(
                                out=aT[:, i, 32 + tg * 512:32 + (tg + 1) * 512], in0=pv[:], in1=s_[:], op=ALU.mult),
                                reads=[pv, s_], pwrites=[aT])
                    for i in range(4):
                        sc.dma("sp", lambda h, i=i: h.dma_start(out=tl_in[i * 128:(i + 1) * 128, :], in_=aT[:, i, TOK:TOK + 32]), aT,
                               reads=[aT], pwrites=[tl_in])
                    prefetch(9, busy=[6, 7])
                    ags = [(tl_in, tl_in.h, tl_ag, tl_ag.h), (lf_in, lf_in.h, lf_ag, lf_ag.h)]
                    for pi in range(4):
                        ags.append((qk_in, qk_in.h[pi * 256:(pi + 1) * 256, :], qk_ag, qk_ag.h[pi * 1024:(pi + 1) * 1024, :]))
                    for pi in range(2):
                        ags.append((v_in, v_in.h[pi * TOK:(pi + 1) * TOK, :], v_ag, v_ag.h[pi * S:(pi + 1) * S, :]))
                    if stop_phase == "A1":
                        ags = []
                    for (srcT, src, dstT, dst) in ags:
                        sc.dma("pool", lambda h, src=src, dst=dst: h.collective_compute(
                            "AllGather", ALU.bypass, replica_groups=GROUPS, ins=[src], outs=[dst], **({"dma_qos": AG_QOS} if AG_QOS else {})),
                            dstT, reads=[srcT] + wt, pwrites=[dstT], inc=1)
                    for gi in range(4):
                        w = wt[(6 + gi) % 4]
                        prefetch(6 + gi + 2, busy=[6 + gi])
                        for i in range(4):
                            for tg in range(4):
                                pg = PS[(cnt * 2) % 6]
                                o_ = ob[cnt % NOB]
                                cnt += 1
                                mm_fm(pg, w, i * 128, tg)
                                sc.op("act", lambda h, pg=pg, o_=o_: h.activation(out=o_[:], in_=pg[:], func=AF.Sigmoid), reads=[pg], writes=[o_])
                                r0 = gi * 512 + i * 128
                                sc.dma("sp", lambda h, o_=o_, r0=r0, tg=tg: h.dma_start(out=sg_sc[r0:r0 + 128, tg * 512:(tg + 1) * 512], in_=o_[:]),
                                       o_, reads=[o_], pwrites=[sg_sc])
                    dump(qk_in, [1024, TOK], BF16)
                    dump(v_in, [2 * TOK, 256], BF16)
                    dump(lf_in, [TOK, 8], F32)
                    sc.emit(bsa, bsb, keep=[qk_ag, v_ag, lf_ag, tl_ag])
                if stop_phase in ("A", "A1"):
                    abscope.close()
                    break
                with contextlib.ExitStack() as st:
                    PS, PSTs = alloc_psum(st, 6, 0)
                    acc = [sb(st, "acc%d" % i, [128, TOK], F32) for i in range(4)]
                    accb = [sb(st, "accb%d" % i, [128, 512], BF16) for i in range(2)]
                    acc2 = [sb(st, "acc2%d" % i, [128, 512], BF16) for i in range(2)]
                    mean = sb(st, "mean", [128, 512], F32)
                    rs = sb(st, "rs", [128, 512], F32)
                    tmp = [sb(st, "tmp%d" % i, [128, 512], F32) for i in range(2)]
                    def hl(h):
                        v = dynv(h, "sp", "halo")
                        return h.dma_start(out=aT[:, :, 0:32], in_=tl_ag.h[bass.ds(v, 512), :].rearrange("(i p) c -> p i c", p=128))
                    sc.dma("sp", hl, aT, reads=[tl_ag], pwrites=[aT])
                    def ldl(h):
                        v = dynv(h, "sp", "lf")
                        return h.dma_start(out=Lt[:], in_=lf_ag.h.rearrange("(b p) e -> p b e", p=128)[:, :, bass.ds(v, 2)])
                    sc.dma("sp", ldl, Lt, reads=[lf_ag], writes=[Lt])
                    for i in range(4):
                        sc.op("dve", lambda h, i=i: h.tensor_scalar(out=aT[:, i, 0:32], in0=aT[:, i, 0:32], scalar1=cm[:, 0:1], scalar2=None, op0=ALU.mult),
                              reads=[aT, cm], writes=[aT] if i == 0 else [], pwrites=[aT] if i > 0 else [])
                    pm, p2 = PS[4], PS[5]
                    bgroups = [(tg, i) for tg in range(4) for i in range(4)]

                    def conv_main(g):
                        tg, i = bgroups[g]
                        pc = PS[g % 4]
                        ab, a2 = accb[g % 2], acc2[g % 2]
                        for k in range(31):
                            sc.op("pe", lambda h, i=i, k=k, pc=pc, tg=tg: h.matmul(pc[:], lhsT=dg[:, i, k, :], rhs=aT[:, i, 2 + k + tg * 512:2 + k + (tg + 1) * 512],
                                                                                   start=(k == 0), stop=(k == 30)),
                                  reads=[dg, aT], writes=([pc] if k == 0 else []), pwrites=([pc] if k > 0 else []))
                        sc.op("act", lambda h, i=i, pc=pc, tg=tg: h.activation(out=acc[i][:, tg * 512:(tg + 1) * 512], in_=pc[:], func=AF.Identity, bias=bdw[:, i:i + 1]),
                              reads=[pc, bdw], pwrites=[acc[i]])
                        sc.op("dve", lambda h, i=i, ab=ab, tg=tg: h.tensor_copy(out=ab[:], in_=acc[i][:, tg * 512:(tg + 1) * 512]),
                              reads=[acc[i]], writes=[ab])
                        sc.op("act", lambda h, i=i, a2=a2, tg=tg: h.activation(out=a2[:], in_=acc[i][:, tg * 512:(tg + 1) * 512], func=AF.Square),
                              reads=[acc[i]], writes=[a2])

                    def conv_stats(g):
                        tg, i = bgroups[g]
                        ab, a2 = accb[g % 2], acc2[g % 2]
                        sc.op("pe", lambda h, i=i, ab=ab, pm=pm: h.matmul(pm[:], lhsT=ones512[:], rhs=ab[:], start=(i == 0), stop=(i == 3)),
                              reads=[ones512, ab], writes=([pm] if i == 0 else []), pwrites=([pm] if i > 0 else []))
                        sc.op("pe", lambda h, i=i, a2=a2, p2=p2: h.matmul(p2[:], lhsT=ones512[:], rhs=a2[:], start=(i == 0), stop=(i == 3)),
                              reads=[ones512, a2], writes=([p2] if i == 0 else []), pwrites=([p2] if i > 0 else []))

                    def conv_ln(tg):
                        sc.op("dve", lambda h, pm=pm: h.tensor_copy(out=mean[:], in_=pm[:]), reads=[pm], writes=[mean])
                        sc.op("dve", lambda h: h.tensor_tensor(out=rs[:], in0=mean[:], in1=mean[:], op=ALU.mult), reads=[mean], writes=[rs])
                        sc.op("dve", lambda h, p2=p2: h.tensor_tensor(out=rs[:], in0=p2[:], in1=rs[:], op=ALU.subtract), reads=[p2, rs], writes=[rs])
                        sc.op("act", lambda h: h.activation(out=rs[:], in_=rs[:], func=AF.Ln, bias=EPS), reads=[rs], writes=[rs])
                        sc.op("act", lambda h: h.activation(out=rs[:], in_=rs[:], func=AF.Exp, scale=-0.5), reads=[rs], writes=[rs])
                        for i in range(4):
                            tm = tmp[i % 2]
                            e = "dve" if i % 2 == 0 else "pool"
                            sc.op(e, lambda h, i=i, tm=tm, tg=tg: h.tensor_tensor(out=tm[:], in0=acc[i][:, tg * 512:(tg + 1) * 512], in1=mean[:], op=ALU.subtract),
                                  reads=[acc[i], mean], writes=[tm])
                            sc.op(e, lambda h, tm=tm: h.tensor_tensor(out=tm[:], in0=tm[:], in1=rs[:], op=ALU.mult), reads=[tm, rs], writes=[tm])
                            sc.op("act", lambda h, i=i, tm=tm, tg=tg: h.activation(out=cvT[:, i, tg * 512:(tg + 1) * 512], in_=tm[:], func=AF.Silu,
                                                                             scale=gln[:, i:i + 1], bias=bln[:, i:i + 1]),
                                  reads=[tm, gln, bln], pwrites=[cvT])

                    for g in range(len(bgroups) + 1):
                        if g < len(bgroups):
                            conv_main(g)
                        if g >= 1:
                            conv_stats(g - 1)
                            if bgroups[g - 1][1] == 3:
                                conv_ln(bgroups[g - 1][0])
                    Lf = Lt.h[:].rearrange("p b e -> p (b e)")
                    sc.op("pe", lambda h: h.matmul(PS[5][:, 0:128], lhsT=trif[:], rhs=Lf, start=True, stop=True), reads=[trif, Lt], writes=[PS[5]])
                    sc.op("pe", lambda h: h.matmul(PS[5][:, 128:256], lhsT=onesf[:], rhs=Lf, start=True, stop=True), reads=[onesf, Lt], pwrites=[PS[5]])
                    sc.op("dve", lambda h: h.tensor_copy(out=ctab[:].rearrange("p b e -> p (b e)"), in_=PS[5][:, 0:128]), reads=[PS[5]], writes=[ctab])
                    sc.op("dve", lambda h: h.tensor_copy(out=Tb[:].rearrange("p b e -> p (b e)"), in_=PS[5][:, 128:256]), reads=[PS[5]], writes=[Tb])
                    sc.op("dve", lambda h: h.tensor_copy(out=Eb[0][:], in_=Tb[:]), reads=[Tb], writes=[Eb[0]])
                    cur = 0
                    for s_ in (1, 2, 4, 8, 16, 32):
                        a_, b_ = Eb[cur], Eb[1 - cur]
                        sc.op("dve", lambda h, a_=a_, b_=b_, s_=s_: h.tensor_copy(out=b_[:, 0:s_, :], in_=a_[:, 0:s_, :]), reads=[a_], writes=[b_])
                        sc.op("dve", lambda h, a_=a_, b_=b_, s_=s_: h.tensor_tensor(out=b_[:, s_:64, :], in0=a_[:, s_:64, :], in1=a_[:, 0:64 - s_, :], op=ALU.add),
                              reads=[a_, b_], writes=[b_])
                        cur = 1 - cur
                    Ein, Eex = Eb[cur], Eb[1 - cur]
                    sc.op("dve", lambda h: h.tensor_tensor(out=Eex[:], in0=Ein[:], in1=Tb[:], op=ALU.subtract), reads=[Ein, Tb], writes=[Eex])
                    sc.op("dve", lambda h: h.tensor_tensor(out=ctab[:], in0=ctab[:], in1=Eex[:], op=ALU.add), reads=[ctab, Eex], writes=[ctab])

                    if debug:
                        for i in range(4):
                            sc.dma("sp", lambda h, i=i: h.dma_start(out=cv_dbg[i * 128:(i + 1) * 128, :], in_=cvT[:, i, :]), cvT, reads=[cvT], pwrites=[cv_dbg])
                        dump(cv_dbg, [512, TOK], BF16)
                    sc.emit(bsa, bsb, keep=[qk_ag, v_ag, lf_ag])
                abscope.close()
                if stop_phase == "B":
                    break
                with contextlib.ExitStack() as st:
                    PS, PSTs = alloc_psum(st, 0, 0)
                    SS = [ps(st, "ss%d_%d" % (l, i), [128, 1024], F32) for i in range(2)]
                    OO = [ps(st, "oo%d_%d" % (l, i), [128, 1024], F32) for i in range(2)]
                    QTs = [sb(st, "QT0", [128, 1, TOK], BF16), sb(st, "QT1", [128, 3, TOK], BF16)]
                    KTs = [sb(st, "KT0", [128, 1, TOK], BF16), sb(st, "KT1", [128, 3, TOK], BF16)]
                    VAs = [sb(st, "VA0", [128, 16, 256], BF16), sb(st, "VA1", [128, 48, 256], BF16)]
                    OT = sb(st, "OT", [128, S], BF16)
                    bias = [sb(st, "bias%d" % i, [128, 64], F32) for i in range(4)]
                    wk = [sb(st, "wk%d" % i, [128, 64], F32) for i in range(4)]
                    NVP = 16
                    VP = [sb(st, "VP%d" % i, [128, 256], BF16) for i in range(NVP)]
                    PT = [sb(st, "PT%d" % i, [128, 1024], BF16) for i in range(3)]
                    Rv = [sb(st, "Rv%d" % i, [128, 512], F32) for i in range(4)]

                    def ldq(h, q, which, part, dst):
                        v = dynv(h, q, "qk")
                        if part == 0:
                            src = qk_ag.h[which * 2048:, :][bass.ds(v, 128), :]
                            return h.dma_start(out=dst[:, 0, :], in_=src)
                        src = qk_ag.h[which * 2048 + 256:, :][bass.ds(v, 768), :].rearrange("(r q) t -> q r t", q=256)[0:128]
                        return h.dma_start(out=dst[:], in_=src)
                    for part in range(2):
                        sc.dma("sp", lambda h, part=part: ldq(h, "sp", 1, part, KTs[part]), KTs[part], reads=[qk_ag], writes=[KTs[part]])
                        sc.dma("pool", lambda h, part=part: ldq(h, "pool", 0, part, QTs[part]), QTs[part], reads=[qk_ag], writes=[QTs[part]])

                    wload(wc, lambda: w_co[l].rearrange("(c p) n -> p c n", p=128), w_co)
                    wload(wa, lambda: w_ao[l].rearrange("(c p) n -> p c n", p=128), w_ao)
                    wload(wo, lambda: w_out[l].rearrange("(c p) n -> p c n", p=128), w_out)
                    VAms = [sc.track(T("VAm0", None)), sc.track(T("VAm1", None))]
                    for part in range(2):
                        for (eng, c0_) in (("pool", 0), ("dve", 192)):
                            sc.op(eng, lambda h, part=part, c0_=c0_: h.memset(VAs[part][:, :, c0_:c0_ + 64], 1.0),
                                  writes=([VAs[part], VAms[part]] if c0_ == 0 else []), pwrites=([] if c0_ == 0 else [VAs[part], VAms[part]]))
                    for part in range(2):
                        def ldv(h, part=part):
                            va = dynv(h, "act", "va")
                            vb = dynv(h, "act", "vb")
                            n = TOK if part == 0 else 3 * TOK
                            src = v_ag.h[part * TOK:, :][bass.ds(va, n), bass.ds(vb, 128)].rearrange("(b p) c -> p b c", p=128)
                            return h.dma_start(out=VAs[part][:, :, 64:192], in_=src)
                        sc.dma("act", ldv, VAs[part], reads=[v_ag, VAms[part]], pwrites=[VAs[part]])
                    steps = []
                    for qt in range(16):
                        nkb = 4 * (qt + 1)
                        for kb in range(nkb):
                            steps.append((qt, kb, nkb))
                    LA = 1

                    def plan_qk(i):
                        qt, kb, nkb = steps[i]
                        c0 = max(0, kb - 4 * qt) * 128
                        n = 512 - c0
                        pss = SS[i % 2]
                        for hh in range(2):
                            hs = slice(hh * 64, (hh + 1) * 64)
                            KT, QT = KTs[min(kb // 16, 1)], QTs[min(qt // 4, 1)]
                            kr, qr = max(kb // 16 - 1, 0), max(qt // 4 - 1, 0)
                            kb_, qt_ = kb % 16, qt % 4
                            sc.op("pe", lambda h, hs=hs, hh=hh, KT=KT, QT=QT, kb_=kb_, qt_=qt_, kr=kr, qr=qr: h.matmul(
                                pss[:, hh * 512:hh * 512 + n], lhsT=KT[hs, kr, kb_ * 128:(kb_ + 1) * 128], rhs=QT[hs, qr, qt_ * 512 + c0:(qt_ + 1) * 512],
                                start=True, stop=True),
                                  reads=[KT, QT], writes=([pss] if hh == 0 else []), pwrites=([pss] if hh == 1 else []))

                    def plan_rest(i):
                        qt, kb, nkb = steps[i]
                        c0 = max(0, kb - 4 * qt) * 128
                        n = 512 - c0
                        pss, pt, vp = SS[i % 2], PT[i % 3], VP[i % NVP]
                        plan_post(i, qt, kb, nkb, c0, n, pss, pt, vp)

                    def plan_vp(i):
                        qt, kb, nkb = steps[i]
                        vp = VP[i % NVP]
                        VA, kb_ = VAs[min(kb // 16, 1)], (kb if kb < 16 else kb - 16)
                        for hh in range(2):
                            bt, wt_ = bias[(qt % 2) * 2 + hh], wk[(qt % 2) * 2 + hh]
                            if kb == 0:
                                sc.op("dve", lambda h, bt=bt, hh=hh: h.tensor_scalar(out=bt[:, 0:nkb], in0=ctab[:, 0:nkb, hh], scalar1=-1.0,
                                                                                     scalar2=Eex[:, 4 * qt + 2, hh:hh + 1], op0=ALU.mult, op1=ALU.add),
                                      reads=[ctab, Eex], writes=[bt])
                                sc.op("act", lambda h, bt=bt, wt_=wt_: h.activation(out=wt_[:, 0:nkb], in_=bt[:, 0:nkb], func=AF.Exp), reads=[bt], writes=[wt_])
                            sc.op("dve", lambda h, hh=hh, wt_=wt_, vp=vp, VA=VA, kb_=kb_: h.tensor_scalar(
                                out=vp[:, hh * 128:(hh + 1) * 128], in0=VA[:, kb_, hh * 128:(hh + 1) * 128], scalar1=wt_[:, kb:kb + 1], scalar2=None, op0=ALU.mult),
                                reads=[VA, wt_], writes=([vp] if hh == 0 else []), pwrites=([vp] if hh == 1 else []))

                    def plan_post(i, qt, kb, nkb, c0, n, pss, pt, vp):
                        sc.op("act", lambda h: h.activation(out=pt[:, 0:512 + n], in_=pss[:, 0:512 + n], func=AF.Exp), reads=[pss], writes=[pt])
                        if kb >= 4 * qt:
                            for hh in range(2):
                                sc.op("pool", lambda h, hh=hh: h.tensor_tensor(out=pt[:, hh * 512:hh * 512 + 128], in0=pt[:, hh * 512:hh * 512 + 128], in1=tri[:], op=ALU.mult),
                                      reads=[pt, tri], writes=[pt])
                        po = OO[qt % 2]
                        for hh in range(2):
                            hs = slice((1 - hh) * 64, (2 - hh) * 64)
                            ls = slice(hh * 64, (hh + 1) * 64)
                            rv = Rv[(qt % 2) * 2 + hh]
                            first = (kb == 0 and hh == 0)
                            sc.op("pe", lambda h, hh=hh: h.matmul(po[:, hh * 512 + c0:(hh + 1) * 512], lhsT=vp[:, hh * 128:(hh + 1) * 128], rhs=pt[:, hh * 512:hh * 512 + n],
                                                                  start=(kb == 0), stop=(kb == nkb - 1)),
                                  reads=[vp, pt], writes=([po] if first else []), pwrites=([] if first else [po]))
                        if kb == nkb - 1:
                            for hh in range(2):
                                hs = slice((1 - hh) * 64, (2 - hh) * 64)
                                ls = slice(hh * 64, (hh + 1) * 64)
                                rv = Rv[(qt % 2) * 2 + hh]
                                sc.op("dve", lambda h, rv=rv, ls=ls, hh=hh: h.reciprocal(out=rv[ls, :], in_=po[ls, hh * 512:(hh + 1) * 512]), reads=[po], writes=[rv])
                                sc.op("dve", lambda h, rv=rv, ls=ls, hs=hs, hh=hh: h.tensor_tensor(out=OT[hs, qt * 512:(qt + 1) * 512], in0=po[hs, hh * 512:(hh + 1) * 512],
                                                                                                    in1=rv[ls, :], op=ALU.mult),
                                      reads=[po, rv], pwrites=[OT])
                        if kb == nkb - 1 and qt in (7, 15):
                            qh = qt // 8
                            for hh in range(2):
                                sc.dma("sp", lambda h, hh=hh: h.dma_start(out=o_in[qh * 128 + hh * 64:qh * 128 + (hh + 1) * 64, :],
                                                                          in_=OT[(1 - hh) * 64:(2 - hh) * 64, qh * 4096:(qh + 1) * 4096]), OT,
                                       reads=[OT], pwrites=[o_in])
                            sc.dma("pool", lambda h: h.collective_compute("AllGather", ALU.bypass, replica_groups=GROUPS,
                                                                          ins=[o_in.h[qh * 128:(qh + 1) * 128, :].rearrange("p (a b) -> (p a) b", b=512)],
                                                                          outs=[o_ag.h[qh * 512:(qh + 1) * 512, :].rearrange("p (a b) -> (p a) b", b=512)]),
                                   o_ag, reads=[o_in], pwrites=[o_ag], inc=1)

                    LAV = 8
                    for i in range(-LAV, len(steps)):
                        if 0 <= i + LAV < len(steps):
                            plan_vp(i + LAV)
                        if 0 <= i + LA < len(steps):
                            plan_qk(i + LA)
                        if i >= 0:
                            plan_rest(i)
                    dump(o_in, [256, S // 2], BF16)
                    sc.emit(bsa, bsb, keep=[o_ag])
                if stop_phase == "C":
                    break
                with contextlib.ExitStack() as st:
                    PS, PSTs = alloc_psum(st, 6, 2)
                    OTo = sb(st, "OTo", [128, 4, TOK], BF16)
                    mT = sb(st, "mT", [128, 8, TOK], BF16)
                    sga = [sb(st, "sga%d" % i, [128, 512], BF16) for i in range(4)]
                    sgb = [sb(st, "sgb%d" % i, [128, 512], BF16) for i in range(4)]
                    m1 = [sb(st, "m1%d" % i, [128, 512], F32) for i in range(2)]
                    m2 = [sb(st, "m2%d" % i, [128, 512], F32) for i in range(2)]
                    xin = [sb(st, "xin%d" % i, [128, D], F32) for i in range(2)]
                    x1 = [sb(st, "x1%d" % i, [128, D], F32) for i in range(3)]
                    hb = [sb(st, "hb%d" % i, [128, D], BF16) for i in range(2)]
                    junk = sb(st, "junk", [128, D], BF16)
                    gB = sb(st, "gB", [128, D], F32)
                    ssq = sb(st, "ssq", [128, NT], F32)
                    lnv = sb(st, "lnv", [128, NT], F32)
                    rstd = sb(st, "rstd", [128, NT], F32)
                    sc.dma("sp", lambda h: h.dma_start(out=gB[:], in_=g_ffn[l:l + 1, :].partition_broadcast(128)), gB, reads=[g_ffn], writes=[gB])

                    def ldo(h):
                        va = dynv(h, "pool", "oa")
                        vb = dynv(h, "pool", "ob")
                        src = o_ag.h[bass.ds(va, 512), bass.ds(vb, TOK)].rearrange("(c p) t -> p c t", p=128)
                        return h.dma_start(out=OTo[:], in_=src)
                    sc.dma("pool", ldo, OTo, reads=[o_ag], writes=[OTo])
                    it = 0
                    for tg in range(4):
                        for fc in range(8):
                            pc = PS[it % 6]
                            a_ = sga[it % 4]
                            it += 1
                            sc.dma("sp", lambda h, a_=a_, fc=fc, tg=tg: h.dma_start(out=a_[:], in_=sg_sc[fc * 128:(fc + 1) * 128, tg * 512:(tg + 1) * 512]),
                                   a_, reads=[sg_sc], writes=[a_])
                            for i in range(4):
                                sc.op("pe", lambda h, i=i, pc=pc, fc=fc, tg=tg: h.matmul(pc[:], lhsT=wc[:, i, fc * 128:(fc + 1) * 128], rhs=cvT[:, i, tg * 512:(tg + 1) * 512],
                                                                                         start=(i == 0), stop=(i == 3)),
                                      reads=[wc, cvT], writes=([pc] if i == 0 else []), pwrites=([pc] if i > 0 else []))
                            sc.op("dve", lambda h, pc=pc, a_=a_, fc=fc, tg=tg: h.tensor_tensor(out=mT[:, fc, tg * 512:(tg + 1) * 512], in0=pc[:], in1=a_[:], op=ALU.mult),
                                  reads=[pc, a_], pwrites=[mT])
                    for tg in range(4):
                        for fc in range(8):
                            pa = PS[it % 6]
                            b_, m2_ = sgb[it % 4], m2[it % 2]
                            it += 1
                            sc.dma("sp", lambda h, b_=b_, fc=fc, tg=tg: h.dma_start(out=b_[:], in_=sg_sc[1024 + fc * 128:1024 + (fc + 1) * 128, tg * 512:(tg + 1) * 512]),
                                   b_, reads=[sg_sc], writes=[b_])
                            for i in range(4):
                                sc.op("pe", lambda h, i=i, pa=pa, fc=fc, tg=tg: h.matmul(pa[:], lhsT=wa[:, i, fc * 128:(fc + 1) * 128], rhs=OTo[:, i, tg * 512:(tg + 1) * 512],
                                                                                         start=(i == 0), stop=(i == 3)),
                                      reads=[wa, OTo], writes=([pa] if i == 0 else []), pwrites=([pa] if i > 0 else []))
                            sc.op("dve", lambda h, pa=pa, b_=b_, m2_=m2_: h.tensor_tensor(out=m2_[:], in0=pa[:], in1=b_[:], op=ALU.mult), reads=[pa, b_], writes=[m2_])
                            sc.op("pool", lambda h, m2_=m2_, fc=fc, tg=tg: h.tensor_tensor(out=mT[:, fc, tg * 512:(tg + 1) * 512], in0=mT[:, fc, tg * 512:(tg + 1) * 512], in1=m2_[:], op=ALU.add),
                                  reads=[m2_, mT], pwrites=[mT])
                    sc.dma("sp", lambda h: h.dma_start(out=xin[0][:], in_=xcur[0:128, :]), xin[0], reads=[xcur], writes=[xin[0]])
                    for t in range(NT):
                        if t + 1 < NT:
                            sc.dma("sp", lambda h, t=t: h.dma_start(out=xin[(t + 1) % 2][:], in_=xcur[(t + 1) * 128:(t + 2) * 128, :]),
                                   xin[(t + 1) % 2], reads=[xcur], writes=[xin[(t + 1) % 2]])
                        xt, x1t = xin[t % 2], x1[t % 3]
                        for cg in range(2):
                            pp = PS[(2 * t + cg) % 4]
                            for c in range(8):
                                sc.op("pe", lambda h, c=c, pp=pp, t=t, cg=cg: h.matmul(pp[:], lhsT=mT[:, c, t * 128:(t + 1) * 128], rhs=wo[:, c, cg * 512:(cg + 1) * 512],
                                                                                       start=(c == 0), stop=(c == 7)),
                                      reads=[mT, wo], writes=([pp] if c == 0 else []), pwrites=([pp] if c > 0 else []))
                            sc.op("dve", lambda h, pp=pp, xt=xt, x1t=x1t, cg=cg: h.tensor_tensor(out=x1t[:, cg * 512:(cg + 1) * 512], in0=pp[:], in1=xt[:, cg * 512:(cg + 1) * 512], op=ALU.add),
                                  reads=[pp, xt], writes=([x1t] if cg == 0 else []), pwrites=([x1t] if cg == 1 else []))
                        sc.dma("sp", lambda h, x1t=x1t, t=t: h.dma_start(out=x1_sc[t * 128:(t + 1) * 128, :], in_=x1t[:]), x1t, reads=[x1t], pwrites=[x1_sc])
                        if t >= 1:
                            norm_transpose(st, t - 1, x1[(t - 1) % 3], gB, ssq, lnv, rstd, junk, hb[(t - 1) % 2])
                    norm_transpose(st, NT - 1, x1[(NT - 1) % 3], gB, ssq, lnv, rstd, junk, hb[(NT - 1) % 2])
                    dump(x1_sc, [TOK, D], F32)
                    sc.emit(bsa, bsb)
            if stop_phase in ("A00", "A01", "A0", "A", "A1", "B", "C", "D"):
                break
            with contextlib.ExitStack() as st:
                PS, PSTs = alloc_psum(st, 6, 0)
                w2 = sb(st, "w2", [128, NFC, D], BF16)
                actT = sb(st, "actT", [128, NFC, 1024], BF16)
                wg = [sb(st, "wg%d" % i, [128, 8, 256], BF16) for i in range(2)]
                wu = [sb(st, "wu%d" % i, [128, 8, 256], BF16) for i in range(2)]
                sl = [sb(st, "sl%d" % i, [128, 512], F32) for i in range(2)]
                xin = [sb(st, "xin%d" % i, [128, D], F32) for i in range(2)]
                xo = [sb(st, "xo%d" % i, [128, D], F32) for i in range(2)]

                def issue_f1(half, g):
                    a, b = wg[(half * 11 + g) % 2], wu[(half * 11 + g) % 2]
                    sc.dma("pool", lambda h: h.dma_start(out=a[:], in_=w_f1[l][:, g * 256:(g + 1) * 256].rearrange("(c p) n -> p c n", p=128)), a, reads=[w_f1], writes=[a])
                    sc.dma("pool", lambda h: h.dma_start(out=b[:], in_=w_f1[l][:, DFF + g * 256:DFF + (g + 1) * 256].rearrange("(c p) n -> p c n", p=128)), b, reads=[w_f1], writes=[b])

                issue_f1(0, 0)

                def issue_w2(fc):
                    sc.dma("pool", lambda h: h.dma_start(out=w2[:, fc, :], in_=w_f2[l][fc * 128:(fc + 1) * 128, :]), w2, reads=[w_f2], pwrites=[w2])
                it = 0
                for half in range(2):
                    for g in range(11):
                        if g + 1 < 11:
                            issue_f1(half, g + 1)
                        elif half == 0:
                            issue_f1(1, 0)
                        if half == 0:
                            issue_w2(2 * g)
                            issue_w2(2 * g + 1)
                        a, b = wg[(half * 11 + g) % 2], wu[(half * 11 + g) % 2]
                        for f2 in range(2):
                            fc = g * 2 + f2
                            for tg in range(2):
                                tok0 = half * 1024 + tg * 512
                                pg, pu = PS[(2 * it) % 6], PS[(2 * it + 1) % 6]
                                s_ = sl[it % 2]
                                it += 1
                                for c in range(8):
                                    sc.op("pe", lambda h, c=c, pg=pg, a=a, f2=f2, tok0=tok0: h.matmul(pg[:], lhsT=a[:, c, f2 * 128:(f2 + 1) * 128], rhs=hTs[tok0 // 512][:, c, :],
                                                                                                      start=(c == 0), stop=(c == 7)),
                                          reads=[a, hTs[tok0 // 512]], writes=([pg] if c == 0 else []), pwrites=([pg] if c > 0 else []))
                                for c in range(8):
                                    sc.op("pe", lambda h, c=c, pu=pu, b=b, f2=f2, tok0=tok0: h.matmul(pu[:], lhsT=b[:, c, f2 * 128:(f2 + 1) * 128], rhs=hTs[tok0 // 512][:, c, :],
                                                                                                      start=(c == 0), stop=(c == 7)),
                                          reads=[b, hTs[tok0 // 512]], writes=([pu] if c == 0 else []), pwrites=([pu] if c > 0 else []))
                                sc.op("act", lambda h, pg=pg, s_=s_: h.activation(out=s_[:], in_=pg[:], func=AF.Silu), reads=[pg], writes=[s_])
                                sc.op("dve", lambda h, pu=pu, s_=s_, fc=fc, tg=tg: h.tensor_tensor(out=actT[:, fc, tg * 512:(tg + 1) * 512], in0=pu[:], in1=s_[:], op=ALU.mult),
                                      reads=[pu, s_], pwrites=[actT])
                    for tt in range(8):
                        t = half * 8 + tt
                        xt, xot = xin[t % 2], xo[t % 2]
                        sc.dma("sp", lambda h, xt=xt, t=t: h.dma_start(out=xt[:], in_=x1_sc[t * 128:(t + 1) * 128, :]), xt, reads=[x1_sc], writes=[xt])
                        for cg in range(2):
                            pp = PS[(2 * t + cg) % 4]
                            for fc in range(NFC):
                                sc.op("pe", lambda h, fc=fc, pp=pp, tt=tt, cg=cg: h.matmul(pp[:], lhsT=actT[:, fc, tt * 128:(tt + 1) * 128], rhs=w2[:, fc, cg * 512:(cg + 1) * 512],
                                                                                           start=(fc == 0), stop=(fc == NFC - 1)),
                                      reads=[actT, w2], writes=([pp] if fc == 0 else []), pwrites=([pp] if fc > 0 else []))
                            sc.op("dve", lambda h, pp=pp, xt=xt, xot=xot, cg=cg: h.tensor_tensor(out=xot[:, cg * 512:(cg + 1) * 512], in0=pp[:], in1=xt[:, cg * 512:(cg + 1) * 512], op=ALU.add),
                                  reads=[pp, xt], writes=([xot] if cg == 0 else []), pwrites=([xot] if cg == 1 else []))
                        sc.dma("sp", lambda h, xot=xot, t=t: h.dma_start(out=xnext[t * 128:(t + 1) * 128, :], in_=xot[:]), xot, reads=[xot], pwrites=[xnext])
                sc.emit(bsa, bsb)
    return nc


_CACHE = {}


def _consts():
    c = np.zeros((128, 256), np.float32)
    c[:, 0:128] = np.eye(128, dtype=np.float32)
    c[:, 128:256] = np.triu(np.ones((128, 128), np.float32))
    return c


def make_in_maps(inputs):
    x = np.ascontiguousarray(inputs["x"], dtype=np.float32)
    consts = _consts()
    maps = []
    for c in range(8):
        b, j = c // 4, c % 4
        m = {k: np.ascontiguousarray(v, dtype=np.float32) for k, v in inputs.items() if k != "x"}
        m["x"] = np.ascontiguousarray(x[b, j * TOK:(j + 1) * TOK, :])
        m["consts"] = consts
        m["cmask"] = np.full((128, 1), 0.0 if j == 0 else 1.0, np.float32)
        maps.append(m)
    return maps


def kernel(**inputs):
    if "nc" not in _CACHE:
        _CACHE["nc"] = build(2, False)
    nc = _CACHE["nc"]
    maps = make_in_maps(inputs)
    res = run_bass_kernel_spmd(nc, maps, core_ids=list(range(8)))
    out = np.zeros((2, S, D), np.float32)
    for c in range(8):
        b, j = c // 4, c % 4
        out[b, j * TOK:(j + 1) * TOK, :] = res.results[c]["y"]
    return out
```

```python
import contextlib
import numpy as np
import concourse.bass as bass
import concourse.mybir as mybir
from concourse.bass_utils import run_bass_kernel_spmd

F32 = mybir.dt.float32
BF16 = mybir.dt.bfloat16
AF = mybir.ActivationFunctionType
ALU = mybir.AluOpType
ENG = ["sp", "act", "dve", "pool", "pe"]
SAME_ENG_SYNC = True

D = 1024
S = 8192
TOK = 2048
NT = 16
DFF = 2816
NFC = 22
INC = 4616
EPS = 1e-6
GROUPS = [[0, 1, 2, 3], [4, 5, 6, 7]]
import os
AG_QOS = os.environ.get("K_AG_QOS", "P3")


class T:
    psum = False

    def __init__(self, name, h, sem=None):
        self.name = name
        self.sem = sem or name
        self.h = h
        self.w = {}
        self.r = {}

    def __getitem__(self, k):
        return self.h[k]


class Sched:
    def __init__(self, nc, stack):
        self.nc = nc
        self.stack = stack
        self.ops = {e: [] for e in ENG}
        self.base = {e: 0 for e in ENG}
        self.esem = {e: stack.enter_context(nc.semaphore("es_" + e)) for e in ENG}
        self.dsem = {}
        self.waited = {e: {} for e in ENG}
        self.bar = stack.enter_context(nc.semaphore("bar"))
        self.nbar = 0
        self.tiles = []
        self.pid = {}

    def track(self, t):
        self.tiles.append(t)
        return t

    def _dsem(self, name):
        if name not in self.dsem:
            self.dsem[name] = [self.stack.enter_context(self.nc.semaphore("ds_" + name)), 0]
        return self.dsem[name]

    def _collect(self, eng, reads, writes, pwrites):
        waits = {}

        def add(d):
            for k, v in d.items():
                if k[0] == "c" and k[1] == eng and (eng == "pe" or not SAME_ENG_SYNC):
                    continue
                if waits.get(k, -1) < v:
                    waits[k] = v

        for t in reads:
            add(t.w)
        for t in writes:
            add(t.w)
            add(t.r)
        for t in pwrites:
            add(t.r)
        return waits

    def _mark(self, waits):
        for k, v in waits.items():
            if k[0] == "c":
                self.ops[k[1]][v]["signal"] = True

    def op(self, eng, fn, reads=(), writes=(), pwrites=()):
        pr = [t for t in reads if t.psum]
        if pr:
            reads = [t for t in reads if not t.psum]
            writes = list(writes) + [t for t in pr if t not in writes]
        waits = self._collect(eng, reads, writes, pwrites)
        self._mark(waits)
        idx = len(self.ops[eng])
        self.ops[eng].append(dict(fn=fn, waits=waits, signal=False, dma=None))
        key = ("c", eng)
        for t in reads:
            t.r[key] = idx
        for t in writes:
            t.w = {key: idx}
            t.r = {}
        for t in pwrites:
            t.w[key] = idx

    def dma(self, q, fn, semtile, reads=(), writes=(), pwrites=(), inc=16):
        waits = self._collect(q, reads, writes, pwrites)
        self._mark(waits)
        ds = self._dsem(semtile.sem)
        ds[1] += inc
        val = ds[1]
        self.ops[q].append(dict(fn=fn, waits=waits, signal=False, dma=(ds[0], inc)))
        key = ("d", semtile.sem)
        for t in reads:
            t.r[key] = val
        for t in writes:
            t.w = {key: val}
            t.r = {}
        for t in pwrites:
            t.w[key] = val

    def emit(self, scratch_a, scratch_b, keep=()):
        nc = self.nc
        prefix = {}
        for e in ENG:
            c = self.base[e]
            pf = []
            for o in self.ops[e]:
                if o["signal"]:
                    c += 1
                pf.append(c)
            prefix[e] = pf
        self.nbar += 1
        nbar = self.nbar
        keep_sems = set(t.sem for t in keep)
        dsem_final = {k: (v[0], v[1]) for k, v in self.dsem.items() if k not in keep_sems}
        last_cnt = {e: (prefix[e][-1] if prefix[e] else self.base[e]) for e in ENG}

        def run(e, h):
            wd = self.waited[e]

            def wait(sem, key, val):
                if wd.get(key, 0) >= val:
                    return
                wd[key] = val
                h.wait_ge(sem, val)

            for o in self.ops[e]:
                for k, v in o["waits"].items():
                    if k[0] == "c":
                        wait(self.esem[k[1]], k, prefix[k[1]][v])
                    else:
                        wait(self.dsem[k[1]][0], k, v)
                ins = o["fn"](h)
                if o["dma"] is not None:
                    ins.then_inc(o["dma"][0], o["dma"][1])
                elif o["signal"]:
                    ins.then_inc(self.esem[e], 1)
            if e == "sp":
                for k, (sem, val) in dsem_final.items():
                    if val > 0:
                        wait(sem, ("d", k), val)
                for e2 in ENG:
                    if e2 != "sp" and last_cnt[e2] > 0:
                        wait(self.esem[e2], ("c", e2), last_cnt[e2])
                h.dma_start(out=scratch_b[:, :], in_=scratch_a[:, :]).then_inc(self.bar, 16)
            h.wait_ge(self.bar, 16 * nbar)

        for e in ENG:
            if e != "sp" and self.ops[e]:
                for o in reversed(self.ops[e]):
                    if o["dma"] is None:
                        if not o["signal"]:
                            o["signal"] = True
                        break
        for e in ENG:
            c = self.base[e]
            pf = []
            for o in self.ops[e]:
                if o["signal"]:
                    c += 1
                pf.append(c)
            prefix[e] = pf
            last_cnt[e] = pf[-1] if pf else self.base[e]

        with nc.Block() as block:
            @block.sync
            def _(h):
                run("sp", h)

            @block.scalar
            def _(h):
                run("act", h)

            @block.vector
            def _(h):
                run("dve", h)

            @block.gpsimd
            def _(h):
                run("pool", h)

            @block.tensor
            def _(h):
                run("pe", h)

        for e in ENG:
            self.base[e] = last_cnt[e]
            self.ops[e] = []
        for t in self.tiles:
            if t in keep:
                t.w = {k: v for k, v in t.w.items() if k[0] == "d"}
                t.r = {}
                continue
            t.w = {}
            t.r = {}


def build(n_layers=2, debug=False, stop_phase=None):
    nc = bass.Bass("TRN2", target_bir_lowering=False)
    stack = contextlib.ExitStack()
    with stack:
        sc = Sched(nc, stack)

        def dram(name, shape, dt, kind=None):
            if kind is None:
                h = nc.dram_tensor(name, shape, dt)
            else:
                h = nc.dram_tensor(name, shape, dt, kind=kind)
            return sc.track(T(name, h.ap()))

        def dbgdram(name, shape, dt):
            return dram(name, shape, dt)

        dbg_done = set()

        def dump(src, shape, dt):
            if not debug or src.name in dbg_done:
                return
            dbg_done.add(src.name)
            dst = dram("d_" + src.name, shape, dt, "ExternalOutput")
            dst.sem = "dbg"
            sc.dma("sp", lambda h: h.dma_start(out=dst.h, in_=src.h), dst, reads=[src], writes=[dst])

        x_in = dram("x", [TOK, D], F32, "ExternalInput")
        g_mix = dram("g_mix", [2, D], F32, "ExternalInput")
        w_in = dram("w_in", [2, D, INC], F32, "ExternalInput")
        b_forget = dram("b_forget", [2, 8], F32, "ExternalInput")
        w_dw = dram("w_dw", [2, 31, 512], F32, "ExternalInput")
        b_dw = dram("b_dw", [2, 512], F32, "ExternalInput")
        g_cln = dram("g_conv_ln", [2, 512], F32, "ExternalInput")
        b_cln = dram("b_conv_ln", [2, 512], F32, "ExternalInput")
        w_co = dram("w_conv_out", [2, 512, D], F32, "ExternalInput")
        g_q = dram("g_q", [2, 64], F32, "ExternalInput")
        g_k = dram("g_k", [2, 64], F32, "ExternalInput")
        w_ao = dram("w_attn_out", [2, 512, D], F32, "ExternalInput")
        w_out = dram("w_out", [2, D, D], F32, "ExternalInput")
        g_ffn = dram("g_ffn", [2, D], F32, "ExternalInput")
        w_f1 = dram("w_ffn_in", [2, D, 2 * DFF], F32, "ExternalInput")
        w_f2 = dram("w_ffn_out", [2, DFF, D], F32, "ExternalInput")
        consts = dram("consts", [128, 256], F32, "ExternalInput")
        cmask = dram("cmask", [128, 1], F32, "ExternalInput")
        y_out = dram("y", [TOK, D], F32, "ExternalOutput")

        qk_in = dbgdram("qk_in", [1024, TOK], BF16)
        qk_ag = dram("qk_ag", [4096 + 256, TOK], BF16)
        v_in = dbgdram("v_in", [2 * TOK, 256], BF16)
        v_ag = dram("v_ag", [2 * S, 256], BF16)
        lf_in = dbgdram("lf_in", [TOK, 8], F32)
        lf_ag = dram("lf_ag", [S, 8], F32)
        tl_in = dram("tl_in", [512, 32], BF16)
        tl_ag = dram("tl_ag", [2048, 32], BF16)
        sg_sc = dram("sg_sc", [2048, TOK], BF16)
        o_in = dbgdram("o_in", [512, TOK], BF16)
        o_ag = dram("o_ag", [2048, TOK], BF16)
        x1_sc = dbgdram("x1_sc", [TOK, D], F32)
        xr_sc = dram("xr_sc", [TOK, D], F32)
        cv_dbg = dram("cv_dbg", [512, TOK], BF16)
        bsa = dram("bsa", [1, 16], F32)
        bsb = dram("bsb", [1, 16], F32)

        uniq = [0]

        def sb(st, name, shape, dt):
            uniq[0] += 1
            return sc.track(T(name, st.enter_context(nc.sbuf_tensor("%s_%d" % (name, uniq[0]), shape, dt)), sem=name))

        def ps(st, name, shape, dt):
            t_ = sc.track(T(name, st.enter_context(nc.psum_tensor(name, shape, dt))))
            t_.psum = True
            return t_

        psn = [0]

        def alloc_psum(st, nf, nt):
            psn[0] += 1
            a = [ps(st, "ps%d_%d" % (psn[0], i), [128, 512], F32) for i in range(nf)]
            b = [ps(st, "pst%d_%d" % (psn[0], i), [128, 8, 128], BF16) for i in range(nt)]
            return a, b

        PS, PSTs = [], []
        ident = sb(stack, "ident", [128, 128], BF16)
        tri = sb(stack, "tri", [128, 128], BF16)
        trif = sb(stack, "trif", [128, 128], F32)
        onesf = sb(stack, "onesf", [128, 128], F32)
        bones = sb(stack, "bones", [128, 128], BF16)
        ones512 = sb(stack, "ones512", [128, 128], BF16)
        cm = sb(stack, "cm", [128, 1], F32)
        hTs = [sb(stack, "hT%d" % i, [128, 8, 512], BF16) for i in range(4)]

        bsa.sem = "dbg"
        sc.dma("sp", lambda h: h.dma_start(out=bsa[:, :], in_=consts[0:1, 0:16]), bsa, reads=[consts], writes=[bsa])
        sc.dma("pool", lambda h: h.dma_start(out=ident[:], in_=consts[:, 0:128]), ident, reads=[consts], writes=[ident])
        sc.dma("pool", lambda h: h.dma_start(out=tri[:], in_=consts[:, 128:256]), tri, reads=[consts], writes=[tri])
        sc.dma("sp", lambda h: h.dma_start(out=trif[:], in_=consts[:, 128:256]), trif, reads=[consts], writes=[trif])
        sc.dma("sp", lambda h: h.dma_start(out=cm[:], in_=cmask[:, :]), cm, reads=[cmask], writes=[cm])
        sc.op("dve", lambda h: h.memset(onesf[:], 1.0), writes=[onesf])
        sc.op("dve", lambda h: h.memset(bones[:], 0.0), writes=[bones])
        sc.op("dve", lambda h: h.memset(bones[0:64, 0:64], 1.0), pwrites=[bones], reads=[bones])
        sc.op("dve", lambda h: h.memset(bones[64:128, 64:128], 1.0), pwrites=[bones], reads=[bones])
        sc.op("dve", lambda h: h.memset(ones512[:], 1.0 / 512.0), writes=[ones512])

        if stop_phase is not None:
            jd = dram("junkd", [20, 8], F32)
            jd.sem = "dbg"
            for ii, tt_ in enumerate([g_mix, w_in, b_forget, w_dw, b_dw, g_cln, b_cln, w_co, g_q, g_k, w_ao, w_out, g_ffn, w_f1, w_f2]):
                flat = tt_.h
                while len(flat.shape) > 1:
                    flat = flat[0]
                sc.dma("sp", lambda h, ii=ii, flat=flat: h.dma_start(out=jd[ii:ii + 1, :], in_=flat[0:8].unsqueeze(0)), jd, reads=[tt_], pwrites=[jd])

        dyn_cache = {}
        DYN = {
            "halo": (lambda j: ((j + 3) % 4) * 512, 1536),
            "qk": (lambda j: (j // 2) * 1024 + (j % 2) * 128, 1152),
            "lf": (lambda j: j * 2, 6),
            "va": (lambda j: (j // 2) * 8192, 8192),
            "vb": (lambda j: (j % 2) * 128, 128),
            "oq": (lambda j: j * 512, 1536),
        }

        def dynv(h, eng, key):
            if (eng, key) not in dyn_cache:
                j = h.partition_id() % 4
                fn, mx = DYN[key]
                dyn_cache[(eng, key)] = h.snap(fn(j), min_val=0, max_val=mx)
            return dyn_cache[(eng, key)]

        def norm_transpose(st, t, xt, gB, ssq, lnv, rstd, junk, hb):
            sc.op("act", lambda h: h.activation(out=junk[:], in_=xt[:], func=AF.Square, accum_out=ssq[:, t:t + 1]),
                  reads=[xt], writes=[junk], pwrites=[ssq])
            sc.op("act", lambda h: h.activation(out=lnv[:, t:t + 1], in_=ssq[:, t:t + 1], func=AF.Ln, scale=1.0 / D, bias=EPS),
                  reads=[ssq], pwrites=[lnv])
            sc.op("act", lambda h: h.activation(out=rstd[:, t:t + 1], in_=lnv[:, t:t + 1], func=AF.Exp, scale=-0.5),
                  reads=[lnv], pwrites=[rstd])
            sc.op("dve", lambda h: h.scalar_tensor_tensor(out=hb[:], in0=xt[:], scalar=rstd[:, t:t + 1], in1=gB[:],
                                                          op0=ALU.mult, op1=ALU.mult),
                  reads=[xt, rstd, gB], writes=[hb])
            PST = PSTs[t % 2]
            for c in range(8):
                sc.op("pe", lambda h, c=c: h.transpose(out=PST[:, c, :], in_=hb[:, c * 128:(c + 1) * 128], identity=ident[:]),
                      reads=[hb, ident], writes=([PST] if c == 0 else []), pwrites=([PST] if c > 0 else []))
            if t % 2 == 0:
                sc.op("act", lambda h: h.activation(out=hTs[t // 4][:, :, (t % 4) * 128:(t % 4 + 1) * 128], in_=PST[:], func=AF.Copy),
                      reads=[PST], pwrites=[hTs[t // 4]])
            else:
                sc.op("dve", lambda h: h.tensor_copy(out=hTs[t // 4][:, :, (t % 4) * 128:(t % 4 + 1) * 128], in_=PST[:]),
                      reads=[PST], pwrites=[hTs[t // 4]])

        def wload(wt, src_ap_fn, srcT):
            sc.dma("pool", lambda h: h.dma_start(out=wt[:], in_=src_ap_fn()), wt, reads=[srcT], writes=[wt])

        if stop_phase == "0":
            sc.emit(bsa, bsb)
            n_layers = 0
        for l in range(n_layers):
            xcur = x_in if l == 0 else xr_sc
            xnext = y_out if l == n_layers - 1 else xr_sc
            with contextlib.ExitStack() as lst:
                cvT = sb(lst, "cvT", [128, 4, TOK], BF16)
                wdw = sb(lst, "wdw", [128, 4, 31], F32)
                bdw = sb(lst, "bdw", [128, 4], F32)
                gln = sb(lst, "gln", [128, 4], F32)
                bln = sb(lst, "bln", [128, 4], F32)
                Lt = sb(lst, "Lt", [128, 64, 2], F32)
                ctab = sb(lst, "ctab", [128, 64, 2], F32)
                Eb = [sb(lst, "Eb%d" % i, [128, 64, 2], F32) for i in range(2)]
                Tb = sb(lst, "Tb", [128, 64, 2], F32)
                wc = sb(lst, "wc", [128, 4, D], BF16)
                wa = sb(lst, "wa", [128, 4, D], BF16)
                wo = sb(lst, "wo", [128, 8, D], BF16)
                abscope = contextlib.ExitStack()
                aT = sb(abscope, "aT", [128, 4, 32 + TOK], BF16)
                dg = sb(abscope, "dg", [128, 4, 31, 128], BF16)
                with contextlib.ExitStack() as st:
                    PS, PSTs = alloc_psum(st, 6, 2)
                    xin = [sb(st, "xin%d" % i, [128, D], F32) for i in range(4)]
                    hb = [sb(st, "hb%d" % i, [128, D], BF16) for i in range(2)]
                    junk = sb(st, "junk", [128, D], BF16)
                    gB = sb(st, "gB", [128, D], F32)
                    ssq = sb(st, "ssq", [128, NT], F32)
                    lnv = sb(st, "lnv", [128, NT], F32)
                    rstd = sb(st, "rstd", [128, NT], F32)
                    wt = [sb(st, "wt%d" % i, [128, 8, 512], BF16) for i in range(4)]
                    gqk = sb(st, "gqk", [128, 2], F32)
                    bfB = sb(st, "bfB", [128, 8], F32)
                    sgt = [sb(st, "sgt%d" % i, [128, 512], F32) for i in range(2)]
                    sqq = [sb(st, "sqq%d" % i, [128, 512], BF16) for i in range(2)]
                    rq = [sb(st, "rq%d" % i, [128, 512], F32) for i in range(2)]
                    NOB = 6
                    ob = [sb(st, "ob%d" % i, [128, 512], BF16) for i in range(NOB)]
                    lft = sb(st, "lft", [128, NT, 8], F32)
                    lfe = sb(st, "lfe", [128, NT, 8], F32)

                    sc.dma("sp", lambda h: h.dma_start(out=xin[0][:], in_=xcur[0:128, :]), xin[0], reads=[xcur], writes=[xin[0]])
                    sc.dma("sp", lambda h: h.dma_start(out=gB[:], in_=g_mix[l:l + 1, :].partition_broadcast(128)), gB,
                           reads=[g_mix], writes=[gB])
                    sc.dma("act", lambda h: h.dma_start(out=bfB[:], in_=b_forget[l:l + 1, :].partition_broadcast(128)), bfB,
                           reads=[b_forget], writes=[bfB])
                    for hh in range(2):
                        sc.dma("act", lambda h, hh=hh: h.dma_start(out=gqk[hh * 64:(hh + 1) * 64, 0:1],
                                                                   in_=g_q[l, :].unsqueeze(1)), gqk, reads=[g_q], pwrites=[gqk])
                        sc.dma("act", lambda h, hh=hh: h.dma_start(out=gqk[hh * 64:(hh + 1) * 64, 1:2],
                                                                   in_=g_k[l, :].unsqueeze(1)), gqk, reads=[g_k], pwrites=[gqk])
                    sc.op("dve", lambda h: h.tensor_scalar(out=gqk[:, 0:1], in0=gqk[:, 0:1], scalar1=0.125, scalar2=None, op0=ALU.mult),
                          reads=[gqk], writes=[gqk])

                    colgroups = [("q", 1024, 512), ("k", 1536, 512), ("v", 2048, 512), ("f", 2560, 8), ("cv", 0, 512), ("cg", 512, 512),
                                 ("ga", 2568, 512), ("ga", 3080, 512), ("gb", 3592, 512), ("gb", 4104, 512)]
                    issued = [0]

                    def issue_w(gi):
                        kind, c0, n = colgroups[gi]
                        w = wt[gi % 4]
                        sc.dma("pool", lambda h: h.dma_start(out=w[:, :, 0:n],
                                                              in_=w_in[l][:, c0:c0 + n].rearrange("(c p) n -> p c n", p=128)),
                               w, reads=[w_in], writes=[w])

                    def prefetch(upto, busy=()):
                        while issued[0] <= min(upto, 9) and (issued[0] % 4) not in [b % 4 for b in busy]:
                            issue_w(issued[0])
                            issued[0] += 1

                    if stop_phase == "A00":
                        sc.emit(bsa, bsb)
                        break
                    prefetch(2)
                    for c4 in range(4):
                        sc.dma("pool", lambda h, c4=c4: h.dma_start(out=wdw[:, c4, :], in_=w_dw[l][:, c4 * 128:(c4 + 1) * 128].rearrange("k p -> p k"),
                                                                  allow_slow_non_contiguous=True), wdw, reads=[w_dw], pwrites=[wdw])
                    for (tt, src) in ((bdw, b_dw), (gln, g_cln), (bln, b_cln)):
                        sc.dma("pool", lambda h, tt=tt, src=src: h.dma_start(out=tt[:], in_=src[l].rearrange("(c p) -> p c", p=128),
                                                                          allow_slow_non_contiguous=True), tt, reads=[src], writes=[tt])
                    if stop_phase == "A01":
                        sc.emit(bsa, bsb)
                        break
                    sc.dma("sp", lambda h: h.dma_start(out=xin[1][:], in_=xcur[128:256, :]), xin[1], reads=[xcur], writes=[xin[1]])
                    sc.dma("sp", lambda h: h.dma_start(out=xin[2][:], in_=xcur[256:384, :]), xin[2], reads=[xcur], writes=[xin[2]])
                    for t in range(NT):
                        if t + 3 < NT:
                            sc.dma("sp", lambda h, t=t: h.dma_start(out=xin[(t + 3) % 4][:], in_=xcur[(t + 3) * 128:(t + 4) * 128, :]),
                                   xin[(t + 3) % 4], reads=[xcur], writes=[xin[(t + 3) % 4]])
                        norm_transpose(st, t, xin[t % 4], gB, ssq, lnv, rstd, junk, hb[t % 2])

                    if stop_phase == "A0":
                        sc.emit(bsa, bsb)
                        break

                    def mm_fm(pst, w, cc0, tg):
                        for c in range(8):
                            sc.op("pe", lambda h, c=c: h.matmul(pst[:], lhsT=w[:, c, cc0:cc0 + 128], rhs=hTs[tg][:, c, :],
                                                                 start=(c == 0), stop=(c == 7)),
                                  reads=[w, hTs[tg]], writes=([pst] if c == 0 else []), pwrites=([pst] if c > 0 else []))

                    def mm_tm(pst, w, n, t):
                        for c in range(8):
                            sc.op("pe", lambda h, c=c: h.matmul(pst[:, 0:n], lhsT=hTs[t // 4][:, c, (t % 4) * 128:(t % 4 + 1) * 128], rhs=w[:, c, 0:n],
                                                                 start=(c == 0), stop=(c == 7)),
                                  reads=[w, hTs[t // 4]], writes=([pst] if c == 0 else []), pwrites=([pst] if c > 0 else []))

                    cnt = 0
                    qk_groups = []
                    for qk in range(2):
                        for i in range(4):
                            for tg in range(4):
                                qk_groups.append((qk, i, tg))
                    ctx = {}

                    def qk_main(g):
                        qk, i, tg = qk_groups[g]
                        nonlocal_cnt = cnt + g
                        pq, p2 = PS[(nonlocal_cnt * 2) % 6], PS[(nonlocal_cnt * 2 + 1) % 6]
                        s_, r_, o_ = sqq[nonlocal_cnt % 2], rq[nonlocal_cnt % 2], ob[nonlocal_cnt % NOB]
                        ctx[g] = (pq, p2, s_, r_, o_)
                        if i == 0 and tg == 0:
                            prefetch(qk + 3, busy=[qk])
                        mm_fm(pq, wt[qk % 4], i * 128, tg)
                        sc.op("act", lambda h, pq=pq, s_=s_: h.activation(out=s_[:], in_=pq[:], func=AF.Square), reads=[pq], writes=[s_])

                    def qk_rest(g):
                        qk, i, tg = qk_groups[g]
                        pq, p2, s_, r_, o_ = ctx[g]
                        sc.op("pe", lambda h, p2=p2, s_=s_: h.matmul(p2[:], lhsT=bones[:], rhs=s_[:], start=True, stop=True),
                              reads=[bones, s_], writes=[p2])
                        sc.op("act", lambda h, p2=p2, r_=r_: h.activation(out=r_[:], in_=p2[:], func=AF.Ln, scale=1.0 / 64, bias=EPS),
                              reads=[p2], writes=[r_])
                        sc.op("act", lambda h, r_=r_: h.activation(out=r_[:], in_=r_[:], func=AF.Exp, scale=-0.5), reads=[r_], writes=[r_])
                        sc.op("dve", lambda h, pq=pq, r_=r_, o_=o_, qk=qk: h.scalar_tensor_tensor(
                            out=o_[:], in0=pq[:], scalar=gqk[:, qk:qk + 1], in1=r_[:], op0=ALU.mult, op1=ALU.mult),
                            reads=[pq, r_, gqk], writes=[o_])
                        r0 = qk * 512 + i * 128
                        sc.dma("sp", lambda h, o_=o_, r0=r0, tg=tg: h.dma_start(out=qk_in[r0:r0 + 128, tg * 512:(tg + 1) * 512], in_=o_[:]),
                               o_, reads=[o_], pwrites=[qk_in])

                    for g in range(len(qk_groups) + 1):
                        if g < len(qk_groups):
                            qk_main(g)
                        if g >= 1:
                            qk_rest(g - 1)
                    cnt += len(qk_groups)
                    for i in range(4):
                        for k in range(31):
                            sc.op("dve", lambda h, i=i, k=k: h.tensor_scalar(out=dg[:, i, k, :], in0=ident[:], scalar1=wdw[:, i, k:k + 1], scalar2=None, op0=ALU.mult),
                                  reads=[ident, wdw], pwrites=[dg])
                    prefetch(5, busy=[2])
                    for t in range(NT):
                        pv = PS[(cnt * 2) % 6]
                        o_ = ob[cnt % NOB]
                        cnt += 1
                        mm_tm(pv, wt[2], 512, t)
                        if t % 2 == 0:
                            sc.op("act", lambda h, pv=pv, o_=o_: h.activation(out=o_[:], in_=pv[:], func=AF.Copy), reads=[pv], writes=[o_])
                        else:
                            sc.op("dve", lambda h, pv=pv, o_=o_: h.tensor_copy(out=o_[:], in_=pv[:]), reads=[pv], writes=[o_])
                        for pi in range(2):
                            sc.dma("sp", lambda h, o_=o_, t=t, pi=pi: h.dma_start(out=v_in[pi * TOK + t * 128:pi * TOK + (t + 1) * 128, :],
                                                                                  in_=o_[:, pi * 256:(pi + 1) * 256]), o_,
                                   reads=[o_], pwrites=[v_in])
                    prefetch(6, busy=[3])
                    for t in range(NT):
                        pf = PS[(cnt * 2) % 6]
                        cnt += 1
                        mm_tm(pf, wt[3], 8, t)
                        sc.op("dve", lambda h, pf=pf, t=t: h.tensor_tensor(out=lft[:, t, :], in0=pf[:, 0:8], in1=bfB[:], op=ALU.add),
                              reads=[pf, bfB], pwrites=[lft])
                    sc.op("act", lambda h: h.activation(out=lfe[:], in_=lft[:], func=AF.Exp, scale=-1.0), reads=[lft], writes=[lfe])
                    sc.op("act", lambda h: h.activation(out=lft[:], in_=lfe[:], func=AF.Ln, bias=1.0), reads=[lfe], writes=[lft])
                    sc.op("dve", lambda h: h.tensor_scalar(out=lfe[:], in0=lft[:], scalar1=-1.0, scalar2=None, op0=ALU.mult), reads=[lft], writes=[lfe])
                    sc.dma("sp", lambda h: h.dma_start(out=lf_in.h.rearrange("(t p) e -> p t e", p=128), in_=lfe[:]), lfe, reads=[lfe], writes=[lf_in])
                    prefetch(7, busy=[4, 5])
                    for i in range(4):
                        for tg in range(4):
                            pv, pg = PS[(cnt * 2) % 6], PS[(cnt * 2 + 1) % 6]
                            s_ = sgt[cnt % 2]
                            cnt += 1
                            mm_fm(pg, wt[5 % 4], i * 128, tg)
                            mm_fm(pv, wt[4 % 4], i * 128, tg)
                            sc.op("act", lambda h, pg=pg, s_=s_: h.activation(out=s_[:], in_=pg[:], func=AF.Sigmoid), reads=[pg], writes=[s_])
                            sc.op("dve", lambda h, pv=pv, s_=s_, i=i, tg=tg: h.tensor_tensor(
                                out=aT[:, i, 32 + tg * 512:32 + (tg + 1) * 512], in0=pv[:], in1=s_[:], op=ALU.mult),
                                reads=[pv, s_], pwrites=[aT])
                    for i in range(4):
                        sc.dma("sp", lambda h, i=i: h.dma_start(out=tl_in[i * 128:(i + 1) * 128, :], in_=aT[:, i, TOK:TOK + 32]), aT,
                               reads=[aT], pwrites=[tl_in])
                    prefetch(9, busy=[6, 7])
                    ags = [(tl_in, tl_in.h, tl_ag, tl_ag.h), (lf_in, lf_in.h, lf_ag, lf_ag.h)]
                    for pi in range(4):
                        ags.append((qk_in, qk_in.h[pi * 256:(pi + 1) * 256, :], qk_ag, qk_ag.h[pi * 1024:(pi + 1) * 1024, :]))
                    for pi in range(2):
                        ags.append((v_in, v_in.h[pi * TOK:(pi + 1) * TOK, :], v_ag, v_ag.h[pi * S:(pi + 1) * S, :]))
                    if stop_phase == "A1":
                        ags = []
                    for (srcT, src, dstT, dst) in ags:
                        sc.dma("pool", lambda h, src=src, dst=dst: h.collective_compute(
                            "AllGather", ALU.bypass, replica_groups=GROUPS, ins=[src], outs=[dst], **({"dma_qos": AG_QOS} if AG_QOS else {})),
                            dstT, reads=[srcT] + wt, pwrites=[dstT], inc=1)
                    for gi in range(4):
                        w = wt[(6 + gi) % 4]
                        prefetch(6 + gi + 2, busy=[6 + gi])
                        for i in range(4):
                            for tg in range(4):
                                pg = PS[(cnt * 2) % 6]
                                o_ = ob[cnt % NOB]
                                cnt += 1
                                mm_fm(pg, w, i * 128, tg)
                                sc.op("act", lambda h, pg=pg, o_=o_: h.activation(out=o_[:], in_=pg[:], func=AF.Sigmoid), reads=[pg], writes=[o_])
                                r0 = gi * 512 + i * 128
                                sc.dma("sp", lambda h, o_=o_, r0=r0, tg=tg: h.dma_start(out=sg_sc[r0:r0 + 128, tg * 512:(tg + 1) * 512], in_=o_[:]),
                                       o_, reads=[o_], pwrites=[sg_sc])
                    dump(qk_in, [1024, TOK], BF16)
                    dump(v_in, [2 * TOK, 256], BF16)
                    dump(lf_in, [TOK, 8], F32)
                    sc.emit(bsa, bsb, keep=[qk_ag, v_ag, lf_ag, tl_ag])
                if stop_phase in ("A", "A1"):
                    abscope.close()
                    break
                with contextlib.ExitStack() as st:
                    PS, PSTs = alloc_psum(st, 6, 0)
                    acc = [sb(st, "acc%d" % i, [128, TOK], F32) for i in range(4)]
                    accb = [sb(st, "accb%d" % i, [128, 512], BF16) for i in range(2)]
                    acc2 = [sb(st, "acc2%d" % i, [128, 512], BF16) for i in range(2)]
                    mean = sb(st, "mean", [128, 512], F32)
                    rs = sb(st, "rs", [128, 512], F32)
                    tmp = [sb(st, "tmp%d" % i, [128, 512], F32) for i in range(2)]
                    def hl(h):
                        v = dynv(h, "sp", "halo")
                        return h.dma_start(out=aT[:, :, 0:32], in_=tl_ag.h[bass.ds(v, 512), :].rearrange("(i p) c -> p i c", p=128))
                    sc.dma("sp", hl, aT, reads=[tl_ag], pwrites=[aT])
                    def ldl(h):
                        v = dynv(h, "sp", "lf")
                        return h.dma_start(out=Lt[:], in_=lf_ag.h.rearrange("(b p) e -> p b e", p=128)[:, :, bass.ds(v, 2)])
                    sc.dma("sp", ldl, Lt, reads=[lf_ag], writes=[Lt])
                    for i in range(4):
                        sc.op("dve", lambda h, i=i: h.tensor_scalar(out=aT[:, i, 0:32], in0=aT[:, i, 0:32], scalar1=cm[:, 0:1], scalar2=None, op0=ALU.mult),
                              reads=[aT, cm], writes=[aT] if i == 0 else [], pwrites=[aT] if i > 0 else [])
                    pm, p2 = PS[4], PS[5]
                    bgroups = [(tg, i) for tg in range(4) for i in range(4)]

                    def conv_main(g):
                        tg, i = bgroups[g]
                        pc = PS[g % 4]
                        ab, a2 = accb[g % 2], acc2[g % 2]
                        for k in range(31):
                            sc.op("pe", lambda h, i=i, k=k, pc=pc, tg=tg: h.matmul(pc[:], lhsT=dg[:, i, k, :], rhs=aT[:, i, 2 + k + tg * 512:2 + k + (tg + 1) * 512],
                                                                                   start=(k == 0), stop=(k == 30)),
                                  reads=[dg, aT], writes=([pc] if k == 0 else []), pwrites=([pc] if k > 0 else []))
                        sc.op("act", lambda h, i=i, pc=pc, tg=tg: h.activation(out=acc[i][:, tg * 512:(tg + 1) * 512], in_=pc[:], func=AF.Identity, bias=bdw[:, i:i + 1]),
                              reads=[pc, bdw], pwrites=[acc[i]])
                        sc.op("dve", lambda h, i=i, ab=ab, tg=tg: h.tensor_copy(out=ab[:], in_=acc[i][:, tg * 512:(tg + 1) * 512]),
                              reads=[acc[i]], writes=[ab])
                        sc.op("act", lambda h, i=i, a2=a2, tg=tg: h.activation(out=a2[:], in_=acc[i][:, tg * 512:(tg + 1) * 512], func=AF.Square),
                              reads=[acc[i]], writes=[a2])

                    def conv_stats(g):
                        tg, i = bgroups[g]
                        ab, a2 = accb[g % 2], acc2[g % 2]
                        sc.op("pe", lambda h, i=i, ab=ab, pm=pm: h.matmul(pm[:], lhsT=ones512[:], rhs=ab[:], start=(i == 0), stop=(i == 3)),
                              reads=[ones512, ab], writes=([pm] if i == 0 else []), pwrites=([pm] if i > 0 else []))
                        sc.op("pe", lambda h, i=i, a2=a2, p2=p2: h.matmul(p2[:], lhsT=ones512[:], rhs=a2[:], start=(i == 0), stop=(i == 3)),
                              reads=[ones512, a2], writes=([p2] if i == 0 else []), pwrites=([p2] if i > 0 else []))

                    def conv_ln(tg):
                        sc.op("dve", lambda h, pm=pm: h.tensor_copy(out=mean[:], in_=pm[:]), reads=[pm], writes=[mean])
                        sc.op("dve", lambda h: h.tensor_tensor(out=rs[:], in0=mean[:], in1=mean[:], op=ALU.mult), reads=[mean], writes=[rs])
                        sc.op("dve", lambda h, p2=p2: h.tensor_tensor(out=rs[:], in0=p2[:], in1=rs[:], op=ALU.subtract), reads=[p2, rs], writes=[rs])
                        sc.op("act", lambda h: h.activation(out=rs[:], in_=rs[:], func=AF.Ln, bias=EPS), reads=[rs], writes=[rs])
                        sc.op("act", lambda h: h.activation(out=rs[:], in_=rs[:], func=AF.Exp, scale=-0.5), reads=[rs], writes=[rs])
                        for i in range(4):
                            tm = tmp[i % 2]
                            e = "dve" if i % 2 == 0 else "pool"
                            sc.op(e, lambda h, i=i, tm=tm, tg=tg: h.tensor_tensor(out=tm[:], in0=acc[i][:, tg * 512:(tg + 1) * 512], in1=mean[:], op=ALU.subtract),
                                  reads=[acc[i], mean], writes=[tm])
                            sc.op(e, lambda h, tm=tm: h.tensor_tensor(out=tm[:], in0=tm[:], in1=rs[:], op=ALU.mult), reads=[tm, rs], writes=[tm])
                            sc.op("act", lambda h, i=i, tm=tm, tg=tg: h.activation(out=cvT[:, i, tg * 512:(tg + 1) * 512], in_=tm[:], func=AF.Silu,
                                                                             scale=gln[:, i:i + 1], bias=bln[:, i:i + 1]),
                                  reads=[tm, gln, bln], pwrites=[cvT])

                    for g in range(len(bgroups) + 1):
                        if g < len(bgroups):
                            conv_main(g)
                        if g >= 1:
                            conv_stats(g - 1)
                            if bgroups[g - 1][1] == 3:
                                conv_ln(bgroups[g - 1][0])
                    Lf = Lt.h[:].rearrange("p b e -> p (b e)")
                    sc.op("pe", lambda h: h.matmul(PS[5][:, 0:128], lhsT=trif[:], rhs=Lf, start=True, stop=True), reads=[trif, Lt], writes=[PS[5]])
                    sc.op("pe", lambda h: h.matmul(PS[5][:, 128:256], lhsT=onesf[:], rhs=Lf, start=True, stop=True), reads=[onesf, Lt], pwrites=[PS[5]])
                    sc.op("dve", lambda h: h.tensor_copy(out=ctab[:].rearrange("p b e -> p (b e)"), in_=PS[5][:, 0:128]), reads=[PS[5]], writes=[ctab])
                    sc.op("dve", lambda h: h.tensor_copy(out=Tb[:].rearrange("p b e -> p (b e)"), in_=PS[5][:, 128:256]), reads=[PS[5]], writes=[Tb])
                    sc.op("dve", lambda h: h.tensor_copy(out=Eb[0][:], in_=Tb[:]), reads=[Tb], writes=[Eb[0]])
                    cur = 0
                    for s_ in (1, 2, 4, 8, 16, 32):
                        a_, b_ = Eb[cur], Eb[1 - cur]
                        sc.op("dve", lambda h, a_=a_, b_=b_, s_=s_: h.tensor_copy(out=b_[:, 0:s_, :], in_=a_[:, 0:s_, :]), reads=[a_], writes=[b_])
                        sc.op("dve", lambda h, a_=a_, b_=b_, s_=s_: h.tensor_tensor(out=b_[:, s_:64, :], in0=a_[:, s_:64, :], in1=a_[:, 0:64 - s_, :], op=ALU.add),
                              reads=[a_, b_], writes=[b_])
                        cur = 1 - cur
                    Ein, Eex = Eb[cur], Eb[1 - cur]
                    sc.op("dve", lambda h: h.tensor_tensor(out=Eex[:], in0=Ein[:], in1=Tb[:], op=ALU.subtract), reads=[Ein, Tb], writes=[Eex])
                    sc.op("dve", lambda h: h.tensor_tensor(out=ctab[:], in0=ctab[:], in1=Eex[:], op=ALU.add), reads=[ctab, Eex], writes=[ctab])

                    if debug:
                        for i in range(4):
                            sc.dma("sp", lambda h, i=i: h.dma_start(out=cv_dbg[i * 128:(i + 1) * 128, :], in_=cvT[:, i, :]), cvT, reads=[cvT], pwrites=[cv_dbg])
                        dump(cv_dbg, [512, TOK], BF16)
                    sc.emit(bsa, bsb, keep=[qk_ag, v_ag, lf_ag])
                abscope.close()
                if stop_phase == "B":
                    break
                with contextlib.ExitStack() as st:
                    PS, PSTs = alloc_psum(st, 0, 0)
                    SS = [ps(st, "ss%d_%d" % (l, i), [128, 1024], F32) for i in range(2)]
                    OO = [ps(st, "oo%d_%d" % (l, i), [128, 1024], F32) for i in range(2)]
                    QTs = [sb(st, "QT0", [128, 1, TOK], BF16), sb(st, "QT1", [128, 3, TOK], BF16)]
                    KTs = [sb(st, "KT0", [128, 1, TOK], BF16), sb(st, "KT1", [128, 3, TOK], BF16)]
                    VAs = [sb(st, "VA0", [128, 16, 256], BF16), sb(st, "VA1", [128, 48, 256], BF16)]
                    OT = sb(st, "OT", [128, S], BF16)
                    bias = [sb(st, "bias%d" % i, [128, 64], F32) for i in range(4)]
                    wk = [sb(st, "wk%d" % i, [128, 64], F32) for i in range(4)]
                    NVP = 16
                    VP = [sb(st, "VP%d" % i, [128, 256], BF16) for i in range(NVP)]
                    PT = [sb(st, "PT%d" % i, [128, 1024], BF16) for i in range(3)]
                    Rv = [sb(st, "Rv%d" % i, [128, 512], F32) for i in range(4)]

                    def ldq(h, q, which, part, dst):
                        v = dynv(h, q, "qk")
                        if part == 0:
                            src = qk_ag.h[which * 2048:, :][bass.ds(v, 128), :]
                            return h.dma_start(out=dst[:, 0, :], in_=src)
                        src = qk_ag.h[which * 2048 + 256:, :][bass.ds(v, 768), :].rearrange("(r q) t -> q r t", q=256)[0:128]
                        return h.dma_start(out=dst[:], in_=src)
                    for part in range(2):
                        sc.dma("sp", lambda h, part=part: ldq(h, "sp", 1, part, KTs[part]), KTs[part], reads=[qk_ag], writes=[KTs[part]])
                        sc.dma("pool", lambda h, part=part: ldq(h, "pool", 0, part, QTs[part]), QTs[part], reads=[qk_ag], writes=[QTs[part]])

                    wload(wc, lambda: w_co[l].rearrange("(c p) n -> p c n", p=128), w_co)
                    wload(wa, lambda: w_ao[l].rearrange("(c p) n -> p c n", p=128), w_ao)
                    wload(wo, lambda: w_out[l].rearrange("(c p) n -> p c n", p=128), w_out)
                    VAms = [sc.track(T("VAm0", None)), sc.track(T("VAm1", None))]
                    for part in range(2):
                        for (eng, c0_) in (("pool", 0), ("dve", 192)):
                            sc.op(eng, lambda h, part=part, c0_=c0_: h.memset(VAs[part][:, :, c0_:c0_ + 64], 1.0),
                                  writes=([VAs[part], VAms[part]] if c0_ == 0 else []), pwrites=([] if c0_ == 0 else [VAs[part], VAms[part]]))
                    for part in range(2):
                        def ldv(h, part=part):
                            va = dynv(h, "act", "va")
                            vb = dynv(h, "act", "vb")
                            n = TOK if part == 0 else 3 * TOK
                            src = v_ag.h[part * TOK:, :][bass.ds(va, n), bass.ds(vb, 128)].rearrange("(b p) c -> p b c", p=128)
                            return h.dma_start(out=VAs[part][:, :, 64:192], in_=src)
                        sc.dma("act", ldv, VAs[part], reads=[v_ag, VAms[part]], pwrites=[VAs[part]])
                    steps = []
                    for qt in range(16):
                        nkb = 4 * (qt + 1)
                        for kb in range(nkb):
                            steps.append((qt, kb, nkb))
                    LA = 1

                    def plan_qk(i):
                        qt, kb, nkb = steps[i]
                        c0 = max(0, kb - 4 * qt) * 128
                        n = 512 - c0
                        pss = SS[i % 2]
                        for hh in range(2):
                            hs = slice(hh * 64, (hh + 1) * 64)
                            KT, QT = KTs[min(kb // 16, 1)], QTs[min(qt // 4, 1)]
                            kr, qr = max(kb // 16 - 1, 0), max(qt // 4 - 1, 0)
                            kb_, qt_ = kb % 16, qt % 4
                            sc.op("pe", lambda h, hs=hs, hh=hh, KT=KT, QT=QT, kb_=kb_, qt_=qt_, kr=kr, qr=qr: h.matmul(
                                pss[:, hh * 512:hh * 512 + n], lhsT=KT[hs, kr, kb_ * 128:(kb_ + 1) * 128], rhs=QT[hs, qr, qt_ * 512 + c0:(qt_ + 1) * 512],
                                start=True, stop=True),
                                  reads=[KT, QT], writes=([pss] if hh == 0 else []), pwrites=([pss] if hh == 1 else []))

                    def plan_rest(i):
                        qt, kb, nkb = steps[i]
                        c0 = max(0, kb - 4 * qt) * 128
                        n = 512 - c0
                        pss, pt, vp = SS[i % 2], PT[i % 3], VP[i % NVP]
                        plan_post(i, qt, kb, nkb, c0, n, pss, pt, vp)

                    def plan_vp(i):
                        qt, kb, nkb = steps[i]
                        vp = VP[i % NVP]
                        VA, kb_ = VAs[min(kb // 16, 1)], (kb if kb < 16 else kb - 16)
                        for hh in range(2):
                            bt, wt_ = bias[(qt % 2) * 2 + hh], wk[(qt % 2) * 2 + hh]
                            if kb == 0:
                                sc.op("dve", lambda h, bt=bt, hh=hh: h.tensor_scalar(out=bt[:, 0:nkb], in0=ctab[:, 0:nkb, hh], scalar1=-1.0,
                                                                                     scalar2=Eex[:, 4 * qt + 2, hh:hh + 1], op0=ALU.mult, op1=ALU.add),
                                      reads=[ctab, Eex], writes=[bt])
                                sc.op("act", lambda h, bt=bt, wt_=wt_: h.activation(out=wt_[:, 0:nkb], in_=bt[:, 0:nkb], func=AF.Exp), reads=[bt], writes=[wt_])
                            sc.op("dve", lambda h, hh=hh, wt_=wt_, vp=vp, VA=VA, kb_=kb_: h.tensor_scalar(
                                out=vp[:, hh * 128:(hh + 1) * 128], in0=VA[:, kb_, hh * 128:(hh + 1) * 128], scalar1=wt_[:, kb:kb + 1], scalar2=None, op0=ALU.mult),
                                reads=[VA, wt_], writes=([vp] if hh == 0 else []), pwrites=([vp] if hh == 1 else []))

                    def plan_post(i, qt, kb, nkb, c0, n, pss, pt, vp):
                        sc.op("act", lambda h: h.activation(out=pt[:, 0:512 + n], in_=pss[:, 0:512 + n], func=AF.Exp), reads=[pss], writes=[pt])
                        if kb >= 4 * qt:
                            for hh in range(2):
                                sc.op("pool", lambda h, hh=hh: h.tensor_tensor(out=pt[:, hh * 512:hh * 512 + 128], in0=pt[:, hh * 512:hh * 512 + 128], in1=tri[:], op=ALU.mult),
                                      reads=[pt, tri], writes=[pt])
                        po = OO[qt % 2]
                        for hh in range(2):
                            hs = slice((1 - hh) * 64, (2 - hh) * 64)
                            ls = slice(hh * 64, (hh + 1) * 64)
                            rv = Rv[(qt % 2) * 2 + hh]
                            first = (kb == 0 and hh == 0)
                            sc.op("pe", lambda h, hh=hh: h.matmul(po[:, hh * 512 + c0:(hh + 1) * 512], lhsT=vp[:, hh * 128:(hh + 1) * 128], rhs=pt[:, hh * 512:hh * 512 + n],
                                                                  start=(kb == 0), stop=(kb == nkb - 1)),
                                  reads=[vp, pt], writes=([po] if first else []), pwrites=([] if first else [po]))
                        if kb == nkb - 1:
                            for hh in range(2):
                                hs = slice((1 - hh) * 64, (2 - hh) * 64)
                                ls = slice(hh * 64, (hh + 1) * 64)
                                rv = Rv[(qt % 2) * 2 + hh]
                                sc.op("dve", lambda h, rv=rv, ls=ls, hh=hh: h.reciprocal(out=rv[ls, :], in_=po[ls, hh * 512:(hh + 1) * 512]), reads=[po], writes=[rv])
                                sc.op("dve", lambda h, rv=rv, ls=ls, hs=hs, hh=hh: h.tensor_tensor(out=OT[hs, qt * 512:(qt + 1) * 512], in0=po[hs, hh * 512:(hh + 1) * 512],
                                                                                                    in1=rv[ls, :], op=ALU.mult),
                                      reads=[po, rv], pwrites=[OT])
                        if kb == nkb - 1 and qt % 4 == 3:
                            qq = qt // 4
                            for hh in range(2):
                                sc.dma("sp", lambda h, hh=hh: h.dma_start(out=o_in[qq * 128 + hh * 64:qq * 128 + (hh + 1) * 64, :],
                                                                          in_=OT[(1 - hh) * 64:(2 - hh) * 64, qq * TOK:(qq + 1) * TOK]), OT,
                                       reads=[OT], pwrites=[o_in])
                            sc.dma("pool", lambda h: h.collective_compute("AllGather", ALU.bypass, replica_groups=GROUPS,
                                                                          ins=[o_in.h[qq * 128:(qq + 1) * 128, :].rearrange("p (a b) -> (p a) b", b=512)],
                                                                          outs=[o_ag.h[qq * 512:(qq + 1) * 512, :].rearrange("p (a b) -> (p a) b", b=512)]),
                                   o_ag, reads=[o_in], pwrites=[o_ag], inc=1)

                    LAV = 8
                    for i in range(-LAV, len(steps)):
                        if 0 <= i + LAV < len(steps):
                            plan_vp(i + LAV)
                        if 0 <= i + LA < len(steps):
                            plan_qk(i + LA)
                        if i >= 0:
                            plan_rest(i)
                    dump(o_in, [512, TOK], BF16)
                    sc.emit(bsa, bsb, keep=[o_ag])
                if stop_phase == "C":
                    break
                with contextlib.ExitStack() as st:
                    PS, PSTs = alloc_psum(st, 6, 2)
                    OTo = sb(st, "OTo", [128, 4, TOK], BF16)
                    mT = sb(st, "mT", [128, 8, TOK], BF16)
                    sga = [sb(st, "sga%d" % i, [128, 512], BF16) for i in range(4)]
                    sgb = [sb(st, "sgb%d" % i, [128, 512], BF16) for i in range(4)]
                    m1 = [sb(st, "m1%d" % i, [128, 512], F32) for i in range(2)]
                    m2 = [sb(st, "m2%d" % i, [128, 512], F32) for i in range(2)]
                    xin = [sb(st, "xin%d" % i, [128, D], F32) for i in range(2)]
                    x1 = [sb(st, "x1%d" % i, [128, D], F32) for i in range(3)]
                    hb = [sb(st, "hb%d" % i, [128, D], BF16) for i in range(2)]
                    junk = sb(st, "junk", [128, D], BF16)
                    gB = sb(st, "gB", [128, D], F32)
                    ssq = sb(st, "ssq", [128, NT], F32)
                    lnv = sb(st, "lnv", [128, NT], F32)
                    rstd = sb(st, "rstd", [128, NT], F32)
                    sc.dma("sp", lambda h: h.dma_start(out=gB[:], in_=g_ffn[l:l + 1, :].partition_broadcast(128)), gB, reads=[g_ffn], writes=[gB])

                    def ldo(h):
                        v = dynv(h, "pool", "oq")
                        src = o_ag.h[bass.ds(v, 512), :].rearrange("(c p) t -> p c t", p=128)
                        return h.dma_start(out=OTo[:], in_=src)
                    sc.dma("pool", ldo, OTo, reads=[o_ag], writes=[OTo])
                    it = 0
                    for tg in range(4):
                        for fc in range(8):
                            pc = PS[it % 6]
                            a_ = sga[it % 4]
                            it += 1
                            sc.dma("sp", lambda h, a_=a_, fc=fc, tg=tg: h.dma_start(out=a_[:], in_=sg_sc[fc * 128:(fc + 1) * 128, tg * 512:(tg + 1) * 512]),
                                   a_, reads=[sg_sc], writes=[a_])
                            for i in range(4):
                                sc.op("pe", lambda h, i=i, pc=pc, fc=fc, tg=tg: h.matmul(pc[:], lhsT=wc[:, i, fc * 128:(fc + 1) * 128], rhs=cvT[:, i, tg * 512:(tg + 1) * 512],
                                                                                         start=(i == 0), stop=(i == 3)),
                                      reads=[wc, cvT], writes=([pc] if i == 0 else []), pwrites=([pc] if i > 0 else []))
                            sc.op("dve", lambda h, pc=pc, a_=a_, fc=fc, tg=tg: h.tensor_tensor(out=mT[:, fc, tg * 512:(tg + 1) * 512], in0=pc[:], in1=a_[:], op=ALU.mult),
                                  reads=[pc, a_], pwrites=[mT])
                    for tg in range(4):
                        for fc in range(8):
                            pa = PS[it % 6]
                            b_, m2_ = sgb[it % 4], m2[it % 2]
                            it += 1
                            sc.dma("sp", lambda h, b_=b_, fc=fc, tg=tg: h.dma_start(out=b_[:], in_=sg_sc[1024 + fc * 128:1024 + (fc + 1) * 128, tg * 512:(tg + 1) * 512]),
                                   b_, reads=[sg_sc], writes=[b_])
                            for i in range(4):
                                sc.op("pe", lambda h, i=i, pa=pa, fc=fc, tg=tg: h.matmul(pa[:], lhsT=wa[:, i, fc * 128:(fc + 1) * 128], rhs=OTo[:, i, tg * 512:(tg + 1) * 512],
                                                                                         start=(i == 0), stop=(i == 3)),
                                      reads=[wa, OTo], writes=([pa] if i == 0 else []), pwrites=([pa] if i > 0 else []))
                            sc.op("dve", lambda h, pa=pa, b_=b_, m2_=m2_: h.tensor_tensor(out=m2_[:], in0=pa[:], in1=b_[:], op=ALU.mult), reads=[pa, b_], writes=[m2_])
                            sc.op("pool", lambda h, m2_=m2_, fc=fc, tg=tg: h.tensor_tensor(out=mT[:, fc, tg * 512:(tg + 1) * 512], in0=mT[:, fc, tg * 512:(tg + 1) * 512], in1=m2_[:], op=ALU.add),
                                  reads=[m2_, mT], pwrites=[mT])
                    sc.dma("sp", lambda h: h.dma_start(out=xin[0][:], in_=xcur[0:128, :]), xin[0], reads=[xcur], writes=[xin[0]])
                    for t in range(NT):
                        if t + 1 < NT:
                            sc.dma("sp", lambda h, t=t: h.dma_start(out=xin[(t + 1) % 2][:], in_=xcur[(t + 1) * 128:(t + 2) * 128, :]),
                                   xin[(t + 1) % 2], reads=[xcur], writes=[xin[(t + 1) % 2]])
                        xt, x1t = xin[t % 2], x1[t % 3]
                        for cg in range(2):
                            pp = PS[(2 * t + cg) % 4]
                            for c in range(8):
                                sc.op("pe", lambda h, c=c, pp=pp, t=t, cg=cg: h.matmul(pp[:], lhsT=mT[:, c, t * 128:(t + 1) * 128], rhs=wo[:, c, cg * 512:(cg + 1) * 512],
                                                                                       start=(c == 0), stop=(c == 7)),
                                      reads=[mT, wo], writes=([pp] if c == 0 else []), pwrites=([pp] if c > 0 else []))
                            sc.op("dve", lambda h, pp=pp, xt=xt, x1t=x1t, cg=cg: h.tensor_tensor(out=x1t[:, cg * 512:(cg + 1) * 512], in0=pp[:], in1=xt[:, cg * 512:(cg + 1) * 512], op=ALU.add),
                                  reads=[pp, xt], writes=([x1t] if cg == 0 else []), pwrites=([x1t] if cg == 1 else []))
                        sc.dma("sp", lambda h, x1t=x1t, t=t: h.dma_start(out=x1_sc[t * 128:(t + 1) * 128, :], in_=x1t[:]), x1t, reads=[x1t], pwrites=[x1_sc])
                        if t >= 1:
                            norm_transpose(st, t - 1, x1[(t - 1) % 3], gB, ssq, lnv, rstd, junk, hb[(t - 1) % 2])
                    norm_transpose(st, NT - 1, x1[(NT - 1) % 3], gB, ssq, lnv, rstd, junk, hb[(NT - 1) % 2])
                    dump(x1_sc, [TOK, D], F32)
                    sc.emit(bsa, bsb)
            if stop_phase in ("A00", "A01", "A0", "A", "A1", "B", "C", "D"):
                break
            with contextlib.ExitStack() as st:
                PS, PSTs = alloc_psum(st, 6, 0)
                w2 = sb(st, "w2", [128, NFC, D], BF16)
                actT = sb(st, "actT", [128, NFC, 1024], BF16)
                wg = [sb(st, "wg%d" % i, [128, 8, 256], BF16) for i in range(2)]
                wu = [sb(st, "wu%d" % i, [128, 8, 256], BF16) for i in range(2)]
                sl = [sb(st, "sl%d" % i, [128, 512], F32) for i in range(2)]
                xin = [sb(st, "xin%d" % i, [128, D], F32) for i in range(2)]
                xo = [sb(st, "xo%d" % i, [128, D], F32) for i in range(2)]

                def issue_f1(half, g):
                    a, b = wg[(half * 11 + g) % 2], wu[(half * 11 + g) % 2]
                    sc.dma("pool", lambda h: h.dma_start(out=a[:], in_=w_f1[l][:, g * 256:(g + 1) * 256].rearrange("(c p) n -> p c n", p=128)), a, reads=[w_f1], writes=[a])
                    sc.dma("pool", lambda h: h.dma_start(out=b[:], in_=w_f1[l][:, DFF + g * 256:DFF + (g + 1) * 256].rearrange("(c p) n -> p c n", p=128)), b, reads=[w_f1], writes=[b])

                issue_f1(0, 0)

                def issue_w2(fc):
                    sc.dma("pool", lambda h: h.dma_start(out=w2[:, fc, :], in_=w_f2[l][fc * 128:(fc + 1) * 128, :]), w2, reads=[w_f2], pwrites=[w2])
                it = 0
                for half in range(2):
                    for g in range(11):
                        if g + 1 < 11:
                            issue_f1(half, g + 1)
                        elif half == 0:
                            issue_f1(1, 0)
                        if half == 0:
                            issue_w2(2 * g)
                            issue_w2(2 * g + 1)
                        a, b = wg[(half * 11 + g) % 2], wu[(half * 11 + g) % 2]
                        for f2 in range(2):
                            fc = g * 2 + f2
                            for tg in range(2):
                                tok0 = half * 1024 + tg * 512
                                pg, pu = PS[(2 * it) % 6], PS[(2 * it + 1) % 6]
                                s_ = sl[it % 2]
                                it += 1
                                for c in range(8):
                                    sc.op("pe", lambda h, c=c, pg=pg, a=a, f2=f2, tok0=tok0: h.matmul(pg[:], lhsT=a[:, c, f2 * 128:(f2 + 1) * 128], rhs=hTs[tok0 // 512][:, c, :],
                                                                                                      start=(c == 0), stop=(c == 7)),
                                          reads=[a, hTs[tok0 // 512]], writes=([pg] if c == 0 else []), pwrites=([pg] if c > 0 else []))
                                for c in range(8):
                                    sc.op("pe", lambda h, c=c, pu=pu, b=b, f2=f2, tok0=tok0: h.matmul(pu[:], lhsT=b[:, c, f2 * 128:(f2 + 1) * 128], rhs=hTs[tok0 // 512][:, c, :],
                                                                                                      start=(c == 0), stop=(c == 7)),
                                          reads=[b, hTs[tok0 // 512]], writes=([pu] if c == 0 else []), pwrites=([pu] if c > 0 else []))
                                sc.op("act", lambda h, pg=pg, s_=s_: h.activation(out=s_[:], in_=pg[:], func=AF.Silu), reads=[pg], writes=[s_])
                                sc.op("dve", lambda h, pu=pu, s_=s_, fc=fc, tg=tg: h.tensor_tensor(out=actT[:, fc, tg * 512:(tg + 1) * 512], in0=pu[:], in1=s_[:], op=ALU.mult),
                                      reads=[pu, s_], pwrites=[actT])
                    for tt in range(8):
                        t = half * 8 + tt
                        xt, xot = xin[t % 2], xo[t % 2]
                        sc.dma("sp", lambda h, xt=xt, t=t: h.dma_start(out=xt[:], in_=x1_sc[t * 128:(t + 1) * 128, :]), xt, reads=[x1_sc], writes=[xt])
                        for cg in range(2):
                            pp = PS[(2 * t + cg) % 4]
                            for fc in range(NFC):
                                sc.op("pe", lambda h, fc=fc, pp=pp, tt=tt, cg=cg: h.matmul(pp[:], lhsT=actT[:, fc, tt * 128:(tt + 1) * 128], rhs=w2[:, fc, cg * 512:(cg + 1) * 512],
                                                                                           start=(fc == 0), stop=(fc == NFC - 1)),
                                      reads=[actT, w2], writes=([pp] if fc == 0 else []), pwrites=([pp] if fc > 0 else []))
                            sc.op("dve", lambda h, pp=pp, xt=xt, xot=xot, cg=cg: h.tensor_tensor(out=xot[:, cg * 512:(cg + 1) * 512], in0=pp[:], in1=xt[:, cg * 512:(cg + 1) * 512], op=ALU.add),
                                  reads=[pp, xt], writes=([xot] if cg == 0 else []), pwrites=([xot] if cg == 1 else []))
                        sc.dma("sp", lambda h, xot=xot, t=t: h.dma_start(out=xnext[t * 128:(t + 1) * 128, :], in_=xot[:]), xot, reads=[xot], pwrites=[xnext])
                sc.emit(bsa, bsb)
    return nc


_CACHE = {}


def _consts():
    c = np.zeros((128, 256), np.float32)
    c[:, 0:128] = np.eye(128, dtype=np.float32)
    c[:, 128:256] = np.triu(np.ones((128, 128), np.float32))
    return c


def make_in_maps(inputs):
    x = np.ascontiguousarray(inputs["x"], dtype=np.float32)
    consts = _consts()
    maps = []
    for c in range(8):
        b, j = c // 4, c % 4
        m = {k: np.ascontiguousarray(v, dtype=np.float32) for k, v in inputs.items() if k != "x"}
        m["x"] = np.ascontiguousarray(x[b, j * TOK:(j + 1) * TOK, :])
        m["consts"] = consts
        m["cmask"] = np.full((128, 1), 0.0 if j == 0 else 1.0, np.float32)
        maps.append(m)
    return maps


def kernel(**inputs):
    if "nc" not in _CACHE:
        _CACHE["nc"] = build(2, False)
    nc = _CACHE["nc"]
    maps = make_in_maps(inputs)
    res = run_bass_kernel_spmd(nc, maps, core_ids=list(range(8)))
    out = np.zeros((2, S, D), np.float32)
    for c in range(8):
        b, j = c // 4, c % 4
        out[b, j * TOK:(j + 1) * TOK, :] = res.results[c]["y"]
    return out
```

```python
import contextlib
import numpy as np
import concourse.bass as bass
import concourse.mybir as mybir
from concourse.bass_utils import run_bass_kernel_spmd

F32 = mybir.dt.float32
BF16 = mybir.dt.bfloat16
AF = mybir.ActivationFunctionType
ALU = mybir.AluOpType
ENG = ["sp", "act", "dve", "pool", "pe"]
SAME_ENG_SYNC = True

D = 1024
S = 8192
TOK = 2048
NT = 16
DFF = 2816
NFC = 22
INC = 4616
EPS = 1e-6
GROUPS = [[0, 1, 2, 3], [4, 5, 6, 7]]
import os
AG_QOS = os.environ.get("K_AG_QOS", "P3")


class T:
    psum = False

    def __init__(self, name, h, sem=None):
        self.name = name
        self.sem = sem or name
        self.h = h
        self.w = {}
        self.r = {}

    def __getitem__(self, k):
        return self.h[k]


class Sched:
    def __init__(self, nc, stack):
        self.nc = nc
        self.stack = stack
        self.ops = {e: [] for e in ENG}
        self.base = {e: 0 for e in ENG}
        self.esem = {e: stack.enter_context(nc.semaphore("es_" + e)) for e in ENG}
        self.dsem = {}
        self.waited = {e: {} for e in ENG}
        self.bar = stack.enter_context(nc.semaphore("bar"))
        self.nbar = 0
        self.tiles = []
        self.pid = {}

    def track(self, t):
        self.tiles.append(t)
        return t

    def _dsem(self, name):
        if name not in self.dsem:
            self.dsem[name] = [self.stack.enter_context(self.nc.semaphore("ds_" + name)), 0]
        return self.dsem[name]

    def _collect(self, eng, reads, writes, pwrites):
        waits = {}

        def add(d):
            for k, v in d.items():
                if k[0] == "c" and k[1] == eng and (eng == "pe" or not SAME_ENG_SYNC):
                    continue
                if waits.get(k, -1) < v:
                    waits[k] = v

        for t in reads:
            add(t.w)
        for t in writes:
            add(t.w)
            add(t.r)
        for t in pwrites:
            add(t.r)
        return waits

    def _mark(self, waits):
        for k, v in waits.items():
            if k[0] == "c":
                self.ops[k[1]][v]["signal"] = True

    def op(self, eng, fn, reads=(), writes=(), pwrites=()):
        pr = [t for t in reads if t.psum]
        if pr:
            reads = [t for t in reads if not t.psum]
            writes = list(writes) + [t for t in pr if t not in writes]
        waits = self._collect(eng, reads, writes, pwrites)
        self._mark(waits)
        idx = len(self.ops[eng])
        self.ops[eng].append(dict(fn=fn, waits=waits, signal=False, dma=None))
        key = ("c", eng)
        for t in reads:
            t.r[key] = idx
        for t in writes:
            t.w = {key: idx}
            t.r = {}
        for t in pwrites:
            t.w[key] = idx

    def dma(self, q, fn, semtile, reads=(), writes=(), pwrites=(), inc=16):
        waits = self._collect(q, reads, writes, pwrites)
        self._mark(waits)
        ds = self._dsem(semtile.sem)
        ds[1] += inc
        val = ds[1]
        self.ops[q].append(dict(fn=fn, waits=waits, signal=False, dma=(ds[0], inc)))
        key = ("d", semtile.sem)
        for t in reads:
            t.r[key] = val
        for t in writes:
            t.w = {key: val}
            t.r = {}
        for t in pwrites:
            t.w[key] = val

    def emit(self, scratch_a, scratch_b, keep=()):
        nc = self.nc
        prefix = {}
        for e in ENG:
            c = self.base[e]
            pf = []
            for o in self.ops[e]:
                if o["signal"]:
                    c += 1
                pf.append(c)
            prefix[e] = pf
        self.nbar += 1
        nbar = self.nbar
        keep_sems = set(t.sem for t in keep)
        dsem_final = {k: (v[0], v[1]) for k, v in self.dsem.items() if k not in keep_sems}
        last_cnt = {e: (prefix[e][-1] if prefix[e] else self.base[e]) for e in ENG}

        def run(e, h):
            wd = self.waited[e]

            def wait(sem, key, val):
                if wd.get(key, 0) >= val:
                    return
                wd[key] = val
                h.wait_ge(sem, val)

            for o in self.ops[e]:
                for k, v in o["waits"].items():
                    if k[0] == "c":
                        wait(self.esem[k[1]], k, prefix[k[1]][v])
                    else:
                        wait(self.dsem[k[1]][0], k, v)
                ins = o["fn"](h)
                if o["dma"] is not None:
                    ins.then_inc(o["dma"][0], o["dma"][1])
                elif o["signal"]:
                    ins.then_inc(self.esem[e], 1)
            if e == "sp":
                for k, (sem, val) in dsem_final.items():
                    if val > 0:
                        wait(sem, ("d", k), val)
                for e2 in ENG:
                    if e2 != "sp" and last_cnt[e2] > 0:
                        wait(self.esem[e2], ("c", e2), last_cnt[e2])
                h.dma_start(out=scratch_b[:, :], in_=scratch_a[:, :]).then_inc(self.bar, 16)
            h.wait_ge(self.bar, 16 * nbar)

        for e in ENG:
            if e != "sp" and self.ops[e]:
                for o in reversed(self.ops[e]):
                    if o["dma"] is None:
                        if not o["signal"]:
                            o["signal"] = True
                        break
        for e in ENG:
            c = self.base[e]
            pf = []
            for o in self.ops[e]:
                if o["signal"]:
                    c += 1
                pf.append(c)
            prefix[e] = pf
            last_cnt[e] = pf[-1] if pf else self.base[e]

        with nc.Block() as block:
            @block.sync
            def _(h):
                run("sp", h)

            @block.scalar
            def _(h):
                run("act", h)

            @block.vector
            def _(h):
                run("dve", h)

            @block.gpsimd
            def _(h):
                run("pool", h)

            @block.tensor
            def _(h):
                run("pe", h)

        for e in ENG:
            self.base[e] = last_cnt[e]
            self.ops[e] = []
        for t in self.tiles:
            if t in keep:
                t.w = {k: v for k, v in t.w.items() if k[0] == "d"}
                t.r = {}
                continue
            t.w = {}
            t.r = {}


def build(n_layers=2, debug=False, stop_phase=None):
    nc = bass.Bass("TRN2", target_bir_lowering=False)
    stack = contextlib.ExitStack()
    with stack:
        sc = Sched(nc, stack)

        def dram(name, shape, dt, kind=None):
            if kind is None:
                h = nc.dram_tensor(name, shape, dt)
            else:
                h = nc.dram_tensor(name, shape, dt, kind=kind)
            return sc.track(T(name, h.ap()))

        def dbgdram(name, shape, dt):
            return dram(name, shape, dt)

        dbg_done = set()

        def dump(src, shape, dt):
            if not debug or src.name in dbg_done:
                return
            dbg_done.add(src.name)
            dst = dram("d_" + src.name, shape, dt, "ExternalOutput")
            dst.sem = "dbg"
            sc.dma("sp", lambda h: h.dma_start(out=dst.h, in_=src.h), dst, reads=[src], writes=[dst])

        x_in = dram("x", [TOK, D], F32, "ExternalInput")
        g_mix = dram("g_mix", [2, D], F32, "ExternalInput")
        w_in = dram("w_in", [2, D, INC], F32, "ExternalInput")
        b_forget = dram("b_forget", [2, 8], F32, "ExternalInput")
        w_dw = dram("w_dw", [2, 31, 512], F32, "ExternalInput")
        b_dw = dram("b_dw", [2, 512], F32, "ExternalInput")
        g_cln = dram("g_conv_ln", [2, 512], F32, "ExternalInput")
        b_cln = dram("b_conv_ln", [2, 512], F32, "ExternalInput")
        w_co = dram("w_conv_out", [2, 512, D], F32, "ExternalInput")
        g_q = dram("g_q", [2, 64], F32, "ExternalInput")
        g_k = dram("g_k", [2, 64], F32, "ExternalInput")
        w_ao = dram("w_attn_out", [2, 512, D], F32, "ExternalInput")
        w_out = dram("w_out", [2, D, D], F32, "ExternalInput")
        g_ffn = dram("g_ffn", [2, D], F32, "ExternalInput")
        w_f1 = dram("w_ffn_in", [2, D, 2 * DFF], F32, "ExternalInput")
        w_f2 = dram("w_ffn_out", [2, DFF, D], F32, "ExternalInput")
        consts = dram("consts", [128, 256], F32, "ExternalInput")
        cmask = dram("cmask", [128, 1], F32, "ExternalInput")
        y_out = dram("y", [TOK, D], F32, "ExternalOutput")

        qk_in = dbgdram("qk_in", [1024, TOK], BF16)
        qk_ag = dram("qk_ag", [4096 + 256, TOK], BF16)
        v_in = dbgdram("v_in", [2 * TOK, 256], BF16)
        v_ag = dram("v_ag", [2 * S, 256], BF16)
        lf_in = dbgdram("lf_in", [TOK, 8], F32)
        lf_ag = dram("lf_ag", [S, 8], F32)
        tl_in = dram("tl_in", [512, 32], BF16)
        tl_ag = dram("tl_ag", [2048, 32], BF16)
        sg_sc = dram("sg_sc", [2048, TOK], BF16)
        o_in = dbgdram("o_in", [512, TOK], BF16)
        o_ag = dram("o_ag", [2048, TOK], BF16)
        x1_sc = dbgdram("x1_sc", [TOK, D], F32)
        xr_sc = dram("xr_sc", [TOK, D], F32)
        cv_dbg = dram("cv_dbg", [512, TOK], BF16)
        bsa = dram("bsa", [1, 16], F32)
        bsb = dram("bsb", [1, 16], F32)

        uniq = [0]

        def sb(st, name, shape, dt):
            uniq[0] += 1
            return sc.track(T(name, st.enter_context(nc.sbuf_tensor("%s_%d" % (name, uniq[0]), shape, dt)), sem=name))

        def ps(st, name, shape, dt):
            t_ = sc.track(T(name, st.enter_context(nc.psum_tensor(name, shape, dt))))
            t_.psum = True
            return t_

        psn = [0]

        def alloc_psum(st, nf, nt):
            psn[0] += 1
            a = [ps(st, "ps%d_%d" % (psn[0], i), [128, 512], F32) for i in range(nf)]
            b = [ps(st, "pst%d_%d" % (psn[0], i), [128, 8, 128], BF16) for i in range(nt)]
            return a, b

        PS, PSTs = [], []
        ident = sb(stack, "ident", [128, 128], BF16)
        tri = sb(stack, "tri", [128, 128], BF16)
        trif = sb(stack, "trif", [128, 128], F32)
        onesf = sb(stack, "onesf", [128, 128], F32)
        bones = sb(stack, "bones", [128, 128], BF16)
        ones512 = sb(stack, "ones512", [128, 128], BF16)
        cm = sb(stack, "cm", [128, 1], F32)
        hTs = [sb(stack, "hT%d" % i, [128, 8, 512], BF16) for i in range(4)]

        bsa.sem = "dbg"
        sc.dma("sp", lambda h: h.dma_start(out=bsa[:, :], in_=consts[0:1, 0:16]), bsa, reads=[consts], writes=[bsa])
        sc.dma("pool", lambda h: h.dma_start(out=ident[:], in_=consts[:, 0:128]), ident, reads=[consts], writes=[ident])
        sc.dma("pool", lambda h: h.dma_start(out=tri[:], in_=consts[:, 128:256]), tri, reads=[consts], writes=[tri])
        sc.dma("sp", lambda h: h.dma_start(out=trif[:], in_=consts[:, 128:256]), trif, reads=[consts], writes=[trif])
        sc.dma("sp", lambda h: h.dma_start(out=cm[:], in_=cmask[:, :]), cm, reads=[cmask], writes=[cm])
        sc.op("dve", lambda h: h.memset(onesf[:], 1.0), writes=[onesf])
        sc.op("dve", lambda h: h.memset(bones[:], 0.0), writes=[bones])
        sc.op("dve", lambda h: h.memset(bones[0:64, 0:64], 1.0), pwrites=[bones], reads=[bones])
        sc.op("dve", lambda h: h.memset(bones[64:128, 64:128], 1.0), pwrites=[bones], reads=[bones])
        sc.op("dve", lambda h: h.memset(ones512[:], 1.0 / 512.0), writes=[ones512])

        if stop_phase is not None:
            jd = dram("junkd", [20, 8], F32)
            jd.sem = "dbg"
            for ii, tt_ in enumerate([g_mix, w_in, b_forget, w_dw, b_dw, g_cln, b_cln, w_co, g_q, g_k, w_ao, w_out, g_ffn, w_f1, w_f2]):
                flat = tt_.h
                while len(flat.shape) > 1:
                    flat = flat[0]
                sc.dma("sp", lambda h, ii=ii, flat=flat: h.dma_start(out=jd[ii:ii + 1, :], in_=flat[0:8].unsqueeze(0)), jd, reads=[tt_], pwrites=[jd])

        dyn_cache = {}
        DYN = {
            "halo": (lambda j: ((j + 3) % 4) * 512, 1536),
            "qk": (lambda j: (j // 2) * 1024 + (j % 2) * 128, 1152),
            "lf": (lambda j: j * 2, 6),
            "va": (lambda j: (j // 2) * 8192, 8192),
            "vb": (lambda j: (j % 2) * 128, 128),
            "oq": (lambda j: j * 512, 1536),
        }

        def dynv(h, eng, key):
            if (eng, key) not in dyn_cache:
                j = h.partition_id() % 4
                fn, mx = DYN[key]
                dyn_cache[(eng, key)] = h.snap(fn(j), min_val=0, max_val=mx)
            return dyn_cache[(eng, key)]

        def norm_transpose(st, t, xt, gB, ssq, lnv, rstd, junk, hb):
            sc.op("act", lambda h: h.activation(out=junk[:], in_=xt[:], func=AF.Square, accum_out=ssq[:, t:t + 1]),
                  reads=[xt], writes=[junk], pwrites=[ssq])
            sc.op("act", lambda h: h.activation(out=lnv[:, t:t + 1], in_=ssq[:, t:t + 1], func=AF.Ln, scale=1.0 / D, bias=EPS),
                  reads=[ssq], pwrites=[lnv])
            sc.op("act", lambda h: h.activation(out=rstd[:, t:t + 1], in_=lnv[:, t:t + 1], func=AF.Exp, scale=-0.5),
                  reads=[lnv], pwrites=[rstd])
            sc.op("dve", lambda h: h.scalar_tensor_tensor(out=hb[:], in0=xt[:], scalar=rstd[:, t:t + 1], in1=gB[:],
                                                          op0=ALU.mult, op1=ALU.mult),
                  reads=[xt, rstd, gB], writes=[hb])
            PST = PSTs[t % 2]
            for c in range(8):
                sc.op("pe", lambda h, c=c: h.transpose(out=PST[:, c, :], in_=hb[:, c * 128:(c + 1) * 128], identity=ident[:]),
                      reads=[hb, ident], writes=([PST] if c == 0 else []), pwrites=([PST] if c > 0 else []))
            if t % 2 == 0:
                sc.op("act", lambda h: h.activation(out=hTs[t // 4][:, :, (t % 4) * 128:(t % 4 + 1) * 128], in_=PST[:], func=AF.Copy),
                      reads=[PST], pwrites=[hTs[t // 4]])
            else:
                sc.op("dve", lambda h: h.tensor_copy(out=hTs[t // 4][:, :, (t % 4) * 128:(t % 4 + 1) * 128], in_=PST[:]),
                      reads=[PST], pwrites=[hTs[t // 4]])

        def wload(wt, src_ap_fn, srcT):
            sc.dma("pool", lambda h: h.dma_start(out=wt[:], in_=src_ap_fn()), wt, reads=[srcT], writes=[wt])

        if stop_phase == "0":
            sc.emit(bsa, bsb)
            n_layers = 0
        for l in range(n_layers):
            xcur = x_in if l == 0 else xr_sc
            xnext = y_out if l == n_layers - 1 else xr_sc
            with contextlib.ExitStack() as lst:
                cvT = sb(lst, "cvT", [128, 4, TOK], BF16)
                wdw = sb(lst, "wdw", [128, 4, 31], F32)
                bdw = sb(lst, "bdw", [128, 4], F32)
                gln = sb(lst, "gln", [128, 4], F32)
                bln = sb(lst, "bln", [128, 4], F32)
                Lt = sb(lst, "Lt", [128, 64, 2], F32)
                ctab = sb(lst, "ctab", [128, 64, 2], F32)
                Eb = [sb(lst, "Eb%d" % i, [128, 64, 2], F32) for i in range(2)]
                Tb = sb(lst, "Tb", [128, 64, 2], F32)
                wc = sb(lst, "wc", [128, 4, D], BF16)
                wa = sb(lst, "wa", [128, 4, D], BF16)
                wo = sb(lst, "wo", [128, 8, D], BF16)
                abscope = contextlib.ExitStack()
                aT = sb(abscope, "aT", [128, 4, 32 + TOK], BF16)
                dg = sb(abscope, "dg", [128, 4, 31, 128], BF16)
                with contextlib.ExitStack() as st:
                    PS, PSTs = alloc_psum(st, 6, 2)
                    xin = [sb(st, "xin%d" % i, [128, D], F32) for i in range(4)]
                    hb = [sb(st, "hb%d" % i, [128, D], BF16) for i in range(2)]
                    junk = sb(st, "junk", [128, D], BF16)
                    gB = sb(st, "gB", [128, D], F32)
                    ssq = sb(st, "ssq", [128, NT], F32)
                    lnv = sb(st, "lnv", [128, NT], F32)
                    rstd = sb(st, "rstd", [128, NT], F32)
                    wt = [sb(st, "wt%d" % i, [128, 8, 512], BF16) for i in range(4)]
                    gqk = sb(st, "gqk", [128, 2], F32)
                    bfB = sb(st, "bfB", [128, 8], F32)
                    sgt = [sb(st, "sgt%d" % i, [128, 512], F32) for i in range(2)]
                    sqq = [sb(st, "sqq%d" % i, [128, 512], BF16) for i in range(2)]
                    rq = [sb(st, "rq%d" % i, [128, 512], F32) for i in range(2)]
                    NOB = 6
                    ob = [sb(st, "ob%d" % i, [128, 512], BF16) for i in range(NOB)]
                    lft = sb(st, "lft", [128, NT, 8], F32)
                    lfe = sb(st, "lfe", [128, NT, 8], F32)

                    sc.dma("sp", lambda h: h.dma_start(out=xin[0][:], in_=xcur[0:128, :]), xin[0], reads=[xcur], writes=[xin[0]])
                    sc.dma("sp", lambda h: h.dma_start(out=gB[:], in_=g_mix[l:l + 1, :].partition_broadcast(128)), gB,
                           reads=[g_mix], writes=[gB])
                    sc.dma("act", lambda h: h.dma_start(out=bfB[:], in_=b_forget[l:l + 1, :].partition_broadcast(128)), bfB,
                           reads=[b_forget], writes=[bfB])
                    for hh in range(2):
                        sc.dma("act", lambda h, hh=hh: h.dma_start(out=gqk[hh * 64:(hh + 1) * 64, 0:1],
                                                                   in_=g_q[l, :].unsqueeze(1)), gqk, reads=[g_q], pwrites=[gqk])
                        sc.dma("act", lambda h, hh=hh: h.dma_start(out=gqk[hh * 64:(hh + 1) * 64, 1:2],
                                                                   in_=g_k[l, :].unsqueeze(1)), gqk, reads=[g_k], pwrites=[gqk])
                    sc.op("dve", lambda h: h.tensor_scalar(out=gqk[:, 0:1], in0=gqk[:, 0:1], scalar1=0.125, scalar2=None, op0=ALU.mult),
                          reads=[gqk], writes=[gqk])

                    colgroups = [("q", 1024, 512), ("k", 1536, 512), ("v", 2048, 512), ("f", 2560, 8), ("cv", 0, 512), ("cg", 512, 512),
                                 ("ga", 2568, 512), ("ga", 3080, 512), ("gb", 3592, 512), ("gb", 4104, 512)]
                    issued = [0]

                    def issue_w(gi):
                        kind, c0, n = colgroups[gi]
                        w = wt[gi % 4]
                        sc.dma("pool", lambda h: h.dma_start(out=w[:, :, 0:n],
                                                              in_=w_in[l][:, c0:c0 + n].rearrange("(c p) n -> p c n", p=128)),
                               w, reads=[w_in], writes=[w])

                    def prefetch(upto, busy=()):
                        while issued[0] <= min(upto, 9) and (issued[0] % 4) not in [b % 4 for b in busy]:
                            issue_w(issued[0])
                            issued[0] += 1

                    if stop_phase == "A00":
                        sc.emit(bsa, bsb)
                        break
                    prefetch(2)
                    for c4 in range(4):
                        sc.dma("pool", lambda h, c4=c4: h.dma_start(out=wdw[:, c4, :], in_=w_dw[l][:, c4 * 128:(c4 + 1) * 128].rearrange("k p -> p k"),
                                                                  allow_slow_non_contiguous=True), wdw, reads=[w_dw], pwrites=[wdw])
                    for (tt, src) in ((bdw, b_dw), (gln, g_cln), (bln, b_cln)):
                        sc.dma("pool", lambda h, tt=tt, src=src: h.dma_start(out=tt[:], in_=src[l].rearrange("(c p) -> p c", p=128),
                                                                          allow_slow_non_contiguous=True), tt, reads=[src], writes=[tt])
                    if stop_phase == "A01":
                        sc.emit(bsa, bsb)
                        break
                    sc.dma("sp", lambda h: h.dma_start(out=xin[1][:], in_=xcur[128:256, :]), xin[1], reads=[xcur], writes=[xin[1]])
                    sc.dma("sp", lambda h: h.dma_start(out=xin[2][:], in_=xcur[256:384, :]), xin[2], reads=[xcur], writes=[xin[2]])
                    for t in range(NT):
                        if t + 3 < NT:
                            sc.dma("sp", lambda h, t=t: h.dma_start(out=xin[(t + 3) % 4][:], in_=xcur[(t + 3) * 128:(t + 4) * 128, :]),
                                   xin[(t + 3) % 4], reads=[xcur], writes=[xin[(t + 3) % 4]])
                        norm_transpose(st, t, xin[t % 4], gB, ssq, lnv, rstd, junk, hb[t % 2])

                    if stop_phase == "A0":
                        sc.emit(bsa, bsb)
                        break

                    def mm_fm(pst, w, cc0, tg):
                        for c in range(8):
                            sc.op("pe", lambda h, c=c: h.matmul(pst[:], lhsT=w[:, c, cc0:cc0 + 128], rhs=hTs[tg][:, c, :],
                                                                 start=(c == 0), stop=(c == 7)),
                                  reads=[w, hTs[tg]], writes=([pst] if c == 0 else []), pwrites=([pst] if c > 0 else []))

                    def mm_tm(pst, w, n, t):
                        for c in range(8):
                            sc.op("pe", lambda h, c=c: h.matmul(pst[:, 0:n], lhsT=hTs[t // 4][:, c, (t % 4) * 128:(t % 4 + 1) * 128], rhs=w[:, c, 0:n],
                                                                 start=(c == 0), stop=(c == 7)),
                                  reads=[w, hTs[t // 4]], writes=([pst] if c == 0 else []), pwrites=([pst] if c > 0 else []))

                    cnt = 0
                    qk_groups = []
                    for qk in range(2):
                        for i in range(4):
                            for tg in range(4):
                                qk_groups.append((qk, i, tg))
                    ctx = {}

                    def qk_main(g):
                        qk, i, tg = qk_groups[g]
                        nonlocal_cnt = cnt + g
                        pq, p2 = PS[(nonlocal_cnt * 2) % 6], PS[(nonlocal_cnt * 2 + 1) % 6]
                        s_, r_, o_ = sqq[nonlocal_cnt % 2], rq[nonlocal_cnt % 2], ob[nonlocal_cnt % NOB]
                        ctx[g] = (pq, p2, s_, r_, o_)
                        if i == 0 and tg == 0:
                            prefetch(qk + 3, busy=[qk])
                        mm_fm(pq, wt[qk % 4], i * 128, tg)
                        sc.op("act", lambda h, pq=pq, s_=s_: h.activation(out=s_[:], in_=pq[:], func=AF.Square), reads=[pq], writes=[s_])

                    def qk_rest(g):
                        qk, i, tg = qk_groups[g]
                        pq, p2, s_, r_, o_ = ctx[g]
                        sc.op("pe", lambda h, p2=p2, s_=s_: h.matmul(p2[:], lhsT=bones[:], rhs=s_[:], start=True, stop=True),
                              reads=[bones, s_], writes=[p2])
                        sc.op("act", lambda h, p2=p2, r_=r_: h.activation(out=r_[:], in_=p2[:], func=AF.Ln, scale=1.0 / 64, bias=EPS),
                              reads=[p2], writes=[r_])
                        sc.op("act", lambda h, r_=r_: h.activation(out=r_[:], in_=r_[:], func=AF.Exp, scale=-0.5), reads=[r_], writes=[r_])
                        sc.op("dve", lambda h, pq=pq, r_=r_, o_=o_, qk=qk: h.scalar_tensor_tensor(
                            out=o_[:], in0=pq[:], scalar=gqk[:, qk:qk + 1], in1=r_[:], op0=ALU.mult, op1=ALU.mult),
                            reads=[pq, r_, gqk], writes=[o_])
                        r0 = qk * 512 + i * 128
                        sc.dma("sp", lambda h, o_=o_, r0=r0, tg=tg: h.dma_start(out=qk_in[r0:r0 + 128, tg * 512:(tg + 1) * 512], in_=o_[:]),
                               o_, reads=[o_], pwrites=[qk_in])

                    for g in range(len(qk_groups) + 1):
                        if g < len(qk_groups):
                            qk_main(g)
                        if g >= 1:
                            qk_rest(g - 1)
                    cnt += len(qk_groups)
                    for i in range(4):
                        for k in range(31):
                            sc.op("dve", lambda h, i=i, k=k: h.tensor_scalar(out=dg[:, i, k, :], in0=ident[:], scalar1=wdw[:, i, k:k + 1], scalar2=None, op0=ALU.mult),
                                  reads=[ident, wdw], pwrites=[dg])
                    prefetch(5, busy=[2])
                    for t in range(NT):
                        pv = PS[(cnt * 2) % 6]
                        o_ = ob[cnt % NOB]
                        cnt += 1
                        mm_tm(pv, wt[2], 512, t)
                        if t % 2 == 0:
                            sc.op("act", lambda h, pv=pv, o_=o_: h.activation(out=o_[:], in_=pv[:], func=AF.Copy), reads=[pv], writes=[o_])
                        else:
                            sc.op("dve", lambda h, pv=pv, o_=o_: h.tensor_copy(out=o_[:], in_=pv[:]), reads=[pv], writes=[o_])
                        for pi in range(2):
                            sc.dma("sp", lambda h, o_=o_, t=t, pi=pi: h.dma_start(out=v_in[pi * TOK + t * 128:pi * TOK + (t + 1) * 128, :],
                                                                                  in_=o_[:, pi * 256:(pi + 1) * 256]), o_,
                                   reads=[o_], pwrites=[v_in])
                    prefetch(6, busy=[3])
                    for t in range(NT):
                        pf = PS[(cnt * 2) % 6]
                        cnt += 1
                        mm_tm(pf, wt[3], 8, t)
                        sc.op("dve", lambda h, pf=pf, t=t: h.tensor_tensor(out=lft[:, t, :], in0=pf[:, 0:8], in1=bfB[:], op=ALU.add),
                              reads=[pf, bfB], pwrites=[lft])
                    sc.op("act", lambda h: h.activation(out=lfe[:], in_=lft[:], func=AF.Exp, scale=-1.0), reads=[lft], writes=[lfe])
                    sc.op("act", lambda h: h.activation(out=lft[:], in_=lfe[:], func=AF.Ln, bias=1.0), reads=[lfe], writes=[lft])
                    sc.op("dve", lambda h: h.tensor_scalar(out=lfe[:], in0=lft[:], scalar1=-1.0, scalar2=None, op0=ALU.mult), reads=[lft], writes=[lfe])
                    sc.dma("sp", lambda h: h.dma_start(out=lf_in.h.rearrange("(t p) e -> p t e", p=128), in_=lfe[:]), lfe, reads=[lfe], writes=[lf_in])
                    prefetch(7, busy=[4, 5])
                    for i in range(4):
                        for tg in range(4):
                            pv, pg = PS[(cnt * 2) % 6], PS[(cnt * 2 + 1) % 6]
                            s_ = sgt[cnt % 2]
                            cnt += 1
                            mm_fm(pg, wt[5 % 4], i * 128, tg)
                            mm_fm(pv, wt[4 % 4], i * 128, tg)
                            sc.op("act", lambda h, pg=pg, s_=s_: h.activation(out=s_[:], in_=pg[:], func=AF.Sigmoid), reads=[pg], writes=[s_])
                            sc.op("dve", lambda h, pv=pv, s_=s_, i=i, tg=tg: h.tensor_tensor(
                                out=aT[:, i, 32 + tg * 512:32 + (tg + 1) * 512], in0=pv[:], in1=s_[:], op=ALU.mult),
                                reads=[pv, s_], pwrites=[aT])
                    for i in range(4):
                        sc.dma("sp", lambda h, i=i: h.dma_start(out=tl_in[i * 128:(i + 1) * 128, :], in_=aT[:, i, TOK:TOK + 32]), aT,
                               reads=[aT], pwrites=[tl_in])
                    prefetch(9, busy=[6, 7])
                    ags = [(tl_in, tl_in.h, tl_ag, tl_ag.h), (lf_in, lf_in.h, lf_ag, lf_ag.h)]
                    for pi in range(4):
                        ags.append((qk_in, qk_in.h[pi * 256:(pi + 1) * 256, :], qk_ag, qk_ag.h[pi * 1024:(pi + 1) * 1024, :]))
                    for pi in range(2):
                        ags.append((v_in, v_in.h[pi * TOK:(pi + 1) * TOK, :], v_ag, v_ag.h[pi * S:(pi + 1) * S, :]))
                    if stop_phase == "A1":
                        ags = []
                    for (srcT, src, dstT, dst) in ags:
                        sc.dma("pool", lambda h, src=src, dst=dst: h.collective_compute(
                            "AllGather", ALU.bypass, replica_groups=GROUPS, ins=[src], outs=[dst], **({"dma_qos": AG_QOS} if AG_QOS else {})),
                            dstT, reads=[srcT] + wt, pwrites=[dstT], inc=1)
                    for gi in range(4):
                        w = wt[(6 + gi) % 4]
                        prefetch(6 + gi + 2, busy=[6 + gi])
                        for i in range(4):
                            for tg in range(4):
                                pg = PS[(cnt * 2) % 6]
                                o_ = ob[cnt % NOB]
                                cnt += 1
                                mm_fm(pg, w, i * 128, tg)
                                sc.op("act", lambda h, pg=pg, o_=o_: h.activation(out=o_[:], in_=pg[:], func=AF.Sigmoid), reads=[pg], writes=[o_])
                                r0 = gi * 512 + i * 128
                                sc.dma("sp", lambda h, o_=o_, r0=r0, tg=tg: h.dma_start(out=sg_sc[r0:r0 + 128, tg * 512:(tg + 1) * 512], in_=o_[:]),
                                       o_, reads=[o_], pwrites=[sg_sc])
                    dump(qk_in, [1024, TOK], BF16)
                    dump(v_in, [2 * TOK, 256], BF16)
                    dump(lf_in, [TOK, 8], F32)
                    sc.emit(bsa, bsb, keep=[qk_ag, v_ag, lf_ag, tl_ag])
                if stop_phase in ("A", "A1"):
                    abscope.close()
                    break
                with contextlib.ExitStack() as st:
                    PS, PSTs = alloc_psum(st, 6, 0)
                    acc = [sb(st, "acc%d" % i, [128, TOK], F32) for i in range(4)]
                    accb = [sb(st, "accb%d" % i, [128, 512], BF16) for i in range(2)]
                    acc2 = [sb(st, "acc2%d" % i, [128, 512], BF16) for i in range(2)]
                    mean = sb(st, "mean", [128, 512], F32)
                    rs = sb(st, "rs", [128, 512], F32)
                    tmp = [sb(st, "tmp%d" % i, [128, 512], F32) for i in range(2)]
                    def hl(h):
                        v = dynv(h, "sp", "halo")
                        return h.dma_start(out=aT[:, :, 0:32], in_=tl_ag.h[bass.ds(v, 512), :].rearrange("(i p) c -> p i c", p=128))
                    sc.dma("sp", hl, aT, reads=[tl_ag], pwrites=[aT])
                    def ldl(h):
                        v = dynv(h, "sp", "lf")
                        return h.dma_start(out=Lt[:], in_=lf_ag.h.rearrange("(b p) e -> p b e", p=128)[:, :, bass.ds(v, 2)])
                    sc.dma("sp", ldl, Lt, reads=[lf_ag], writes=[Lt])
                    for i in range(4):
                        sc.op("dve", lambda h, i=i: h.tensor_scalar(out=aT[:, i, 0:32], in0=aT[:, i, 0:32], scalar1=cm[:, 0:1], scalar2=None, op0=ALU.mult),
                              reads=[aT, cm], writes=[aT] if i == 0 else [], pwrites=[aT] if i > 0 else [])
                    pm, p2 = PS[4], PS[5]
                    bgroups = [(tg, i) for tg in range(4) for i in range(4)]

                    def conv_main(g):
                        tg, i = bgroups[g]
                        pc = PS[g % 4]
                        ab, a2 = accb[g % 2], acc2[g % 2]
                        for k in range(31):
                            sc.op("pe", lambda h, i=i, k=k, pc=pc, tg=tg: h.matmul(pc[:], lhsT=dg[:, i, k, :], rhs=aT[:, i, 2 + k + tg * 512:2 + k + (tg + 1) * 512],
                                                                                   start=(k == 0), stop=(k == 30)),
                                  reads=[dg, aT], writes=([pc] if k == 0 else []), pwrites=([pc] if k > 0 else []))
                        sc.op("act", lambda h, i=i, pc=pc, tg=tg: h.activation(out=acc[i][:, tg * 512:(tg + 1) * 512], in_=pc[:], func=AF.Identity, bias=bdw[:, i:i + 1]),
                              reads=[pc, bdw], pwrites=[acc[i]])
                        sc.op("dve", lambda h, i=i, ab=ab, tg=tg: h.tensor_copy(out=ab[:], in_=acc[i][:, tg * 512:(tg + 1) * 512]),
                              reads=[acc[i]], writes=[ab])
                        sc.op("act", lambda h, i=i, a2=a2, tg=tg: h.activation(out=a2[:], in_=acc[i][:, tg * 512:(tg + 1) * 512], func=AF.Square),
                              reads=[acc[i]], writes=[a2])

                    def conv_stats(g):
                        tg, i = bgroups[g]
                        ab, a2 = accb[g % 2], acc2[g % 2]
                        sc.op("pe", lambda h, i=i, ab=ab, pm=pm: h.matmul(pm[:], lhsT=ones512[:], rhs=ab[:], start=(i == 0), stop=(i == 3)),
                              reads=[ones512, ab], writes=([pm] if i == 0 else []), pwrites=([pm] if i > 0 else []))
                        sc.op("pe", lambda h, i=i, a2=a2, p2=p2: h.matmul(p2[:], lhsT=ones512[:], rhs=a2[:], start=(i == 0), stop=(i == 3)),
                              reads=[ones512, a2], writes=([p2] if i == 0 else []), pwrites=([p2] if i > 0 else []))

                    def conv_ln(tg):
                        sc.op("dve", lambda h, pm=pm: h.tensor_copy(out=mean[:], in_=pm[:]), reads=[pm], writes=[mean])
                        sc.op("dve", lambda h: h.tensor_tensor(out=rs[:], in0=mean[:], in1=mean[:], op=ALU.mult), reads=[mean], writes=[rs])
                        sc.op("dve", lambda h, p2=p2: h.tensor_tensor(out=rs[:], in0=p2[:], in1=rs[:], op=ALU.subtract), reads=[p2, rs], writes=[rs])
                        sc.op("act", lambda h: h.activation(out=rs[:], in_=rs[:], func=AF.Ln, bias=EPS), reads=[rs], writes=[rs])
                        sc.op("act", lambda h: h.activation(out=rs[:], in_=rs[:], func=AF.Exp, scale=-0.5), reads=[rs], writes=[rs])
                        for i in range(4):
                            tm = tmp[i % 2]
                            e = "dve" if i % 2 == 0 else "pool"
                            sc.op(e, lambda h, i=i, tm=tm, tg=tg: h.tensor_tensor(out=tm[:], in0=acc[i][:, tg * 512:(tg + 1) * 512], in1=mean[:], op=ALU.subtract),
                                  reads=[acc[i], mean], writes=[tm])
                            sc.op(e, lambda h, tm=tm: h.tensor_tensor(out=tm[:], in0=tm[:], in1=rs[:], op=ALU.mult), reads=[tm, rs], writes=[tm])
                            sc.op("act", lambda h, i=i, tm=tm, tg=tg: h.activation(out=cvT[:, i, tg * 512:(tg + 1) * 512], in_=tm[:], func=AF.Silu,
                                                                             scale=gln[:, i:i + 1], bias=bln[:, i:i + 1]),
                                  reads=[tm, gln, bln], pwrites=[cvT])

                    for g in range(len(bgroups) + 1):
                        if g < len(bgroups):
                            conv_main(g)
                        if g >= 1:
                            conv_stats(g - 1)
                            if bgroups[g - 1][1] == 3:
                                conv_ln(bgroups[g - 1][0])
                    Lf = Lt.h[:].rearrange("p b e -> p (b e)")
                    sc.op("pe", lambda h: h.matmul(PS[5][:, 0:128], lhsT=trif[:], rhs=Lf, start=True, stop=True), reads=[trif, Lt], writes=[PS[5]])
                    sc.op("pe", lambda h: h.matmul(PS[5][:, 128:256], lhsT=onesf[:], rhs=Lf, start=True, stop=True), reads=[onesf, Lt], pwrites=[PS[5]])
                    sc.op("dve", lambda h: h.tensor_copy(out=ctab[:].rearrange("p b e -> p (b e)"), in_=PS[5][:, 0:128]), reads=[PS[5]], writes=[ctab])
                    sc.op("dve", lambda h: h.tensor_copy(out=Tb[:].rearrange("p b e -> p (b e)"), in_=PS[5][:, 128:256]), reads=[PS[5]], writes=[Tb])
                    sc.op("dve", lambda h: h.tensor_copy(out=Eb[0][:], in_=Tb[:]), reads=[Tb], writes=[Eb[0]])
                    cur = 0
                    for s_ in (1, 2, 4, 8, 16, 32):
                        a_, b_ = Eb[cur], Eb[1 - cur]
                        sc.op("dve", lambda h, a_=a_, b_=b_, s_=s_: h.tensor_copy(out=b_[:, 0:s_, :], in_=a_[:, 0:s_, :]), reads=[a_], writes=[b_])
                        sc.op("dve", lambda h, a_=a_, b_=b_, s_=s_: h.tensor_tensor(out=b_[:, s_:64, :], in0=a_[:, s_:64, :], in1=a_[:, 0:64 - s_, :], op=ALU.add),
                              reads=[a_, b_], writes=[b_])
                        cur = 1 - cur
                    Ein, Eex = Eb[cur], Eb[1 - cur]
                    sc.op("dve", lambda h: h.tensor_tensor(out=Eex[:], in0=Ein[:], in1=Tb[:], op=ALU.subtract), reads=[Ein, Tb], writes=[Eex])
                    sc.op("dve", lambda h: h.tensor_tensor(out=ctab[:], in0=ctab[:], in1=Eex[:], op=ALU.add), reads=[ctab, Eex], writes=[ctab])

                    if debug:
                        for i in range(4):
                            sc.dma("sp", lambda h, i=i: h.dma_start(out=cv_dbg[i * 128:(i + 1) * 128, :], in_=cvT[:, i, :]), cvT, reads=[cvT], pwrites=[cv_dbg])
                        dump(cv_dbg, [512, TOK], BF16)
                    sc.emit(bsa, bsb, keep=[qk_ag, v_ag, lf_ag])
                abscope.close()
                if stop_phase == "B":
                    break
                with contextlib.ExitStack() as st:
                    PS, PSTs = alloc_psum(st, 0, 0)
                    SS = [ps(st, "ss%d_%d" % (l, i), [128, 1024], F32) for i in range(2)]
                    OO = [ps(st, "oo%d_%d" % (l, i), [128, 1024], F32) for i in range(2)]
                    QTs = [sb(st, "QT0", [128, 1, TOK], BF16), sb(st, "QT1", [128, 3, TOK], BF16)]
                    KTs = [sb(st, "KT0", [128, 1, TOK], BF16), sb(st, "KT1", [128, 3, TOK], BF16)]
                    VAs = [sb(st, "VA0", [128, 16, 256], BF16), sb(st, "VA1", [128, 48, 256], BF16)]
                    OT = sb(st, "OT", [128, S], BF16)
                    bias = [sb(st, "bias%d" % i, [128, 64], F32) for i in range(4)]
                    wk = [sb(st, "wk%d" % i, [128, 64], F32) for i in range(4)]
                    NVP = 16
                    VP = [sb(st, "VP%d" % i, [128, 256], BF16) for i in range(NVP)]
                    PT = [sb(st, "PT%d" % i, [128, 1024], BF16) for i in range(3)]
                    Rv = [sb(st, "Rv%d" % i, [128, 512], F32) for i in range(4)]

                    def ldq(h, q, which, part, dst):
                        v = dynv(h, q, "qk")
                        if part == 0:
                            src = qk_ag.h[which * 2048:, :][bass.ds(v, 128), :]
                            return h.dma_start(out=dst[:, 0, :], in_=src)
                        src = qk_ag.h[which * 2048 + 256:, :][bass.ds(v, 768), :].rearrange("(r q) t -> q r t", q=256)[0:128]
                        return h.dma_start(out=dst[:], in_=src)
                    for part in range(2):
                        sc.dma("sp", lambda h, part=part: ldq(h, "sp", 1, part, KTs[part]), KTs[part], reads=[qk_ag], writes=[KTs[part]])
                        sc.dma("pool", lambda h, part=part: ldq(h, "pool", 0, part, QTs[part]), QTs[part], reads=[qk_ag], writes=[QTs[part]])

                    wload(wc, lambda: w_co[l].rearrange("(c p) n -> p c n", p=128), w_co)
                    wload(wa, lambda: w_ao[l].rearrange("(c p) n -> p c n", p=128), w_ao)
                    wload(wo, lambda: w_out[l].rearrange("(c p) n -> p c n", p=128), w_out)
                    VAms = [sc.track(T("VAm0", None)), sc.track(T("VAm1", None))]
                    for part in range(2):
                        for (eng, c0_) in (("pool", 0), ("dve", 192)):
                            sc.op(eng, lambda h, part=part, c0_=c0_: h.memset(VAs[part][:, :, c0_:c0_ + 64], 1.0),
                                  writes=([VAs[part], VAms[part]] if c0_ == 0 else []), pwrites=([] if c0_ == 0 else [VAs[part], VAms[part]]))
                    for part in range(2):
                        def ldv(h, part=part):
                            va = dynv(h, "act", "va")
                            vb = dynv(h, "act", "vb")
                            n = TOK if part == 0 else 3 * TOK
                            src = v_ag.h[part * TOK:, :][bass.ds(va, n), bass.ds(vb, 128)].rearrange("(b p) c -> p b c", p=128)
                            return h.dma_start(out=VAs[part][:, :, 64:192], in_=src)
                        sc.dma("act", ldv, VAs[part], reads=[v_ag, VAms[part]], pwrites=[VAs[part]])
                    steps = []
                    for qt in range(16):
                        nkb = 4 * (qt + 1)
                        for kb in range(nkb):
                            steps.append((qt, kb, nkb))
                    LA = 1

                    def plan_qk(i):
                        qt, kb, nkb = steps[i]
                        c0 = max(0, kb - 4 * qt) * 128
                        n = 512 - c0
                        pss = SS[i % 2]
                        for hh in range(2):
                            hs = slice(hh * 64, (hh + 1) * 64)
                            KT, QT = KTs[min(kb // 16, 1)], QTs[min(qt // 4, 1)]
                            kr, qr = max(kb // 16 - 1, 0), max(qt // 4 - 1, 0)
                            kb_, qt_ = kb % 16, qt % 4
                            sc.op("pe", lambda h, hs=hs, hh=hh, KT=KT, QT=QT, kb_=kb_, qt_=qt_, kr=kr, qr=qr: h.matmul(
                                pss[:, hh * 512:hh * 512 + n], lhsT=KT[hs, kr, kb_ * 128:(kb_ + 1) * 128], rhs=QT[hs, qr, qt_ * 512 + c0:(qt_ + 1) * 512],
                                start=True, stop=True),
                                  reads=[KT, QT], writes=([pss] if hh == 0 else []), pwrites=([pss] if hh == 1 else []))

                    def plan_rest(i):
                        qt, kb, nkb = steps[i]
                        c0 = max(0, kb - 4 * qt) * 128
                        n = 512 - c0
                        pss, pt, vp = SS[i % 2], PT[i % 3], VP[i % NVP]
                        plan_post(i, qt, kb, nkb, c0, n, pss, pt, vp)

                    def plan_vp(i):
                        qt, kb, nkb = steps[i]
                        vp = VP[i % NVP]
                        VA, kb_ = VAs[min(kb // 16, 1)], (kb if kb < 16 else kb - 16)
                        for hh in range(2):
                            bt, wt_ = bias[(qt % 2) * 2 + hh], wk[(qt % 2) * 2 + hh]
                            if kb == 0:
                                sc.op("dve", lambda h, bt=bt, hh=hh: h.tensor_scalar(out=bt[:, 0:nkb], in0=ctab[:, 0:nkb, hh], scalar1=-1.0,
                                                                                     scalar2=Eex[:, 4 * qt + 2, hh:hh + 1], op0=ALU.mult, op1=ALU.add),
                                      reads=[ctab, Eex], writes=[bt])
                                sc.op("act", lambda h, bt=bt, wt_=wt_: h.activation(out=wt_[:, 0:nkb], in_=bt[:, 0:nkb], func=AF.Exp), reads=[bt], writes=[wt_])
                            sc.op("dve", lambda h, hh=hh, wt_=wt_, vp=vp, VA=VA, kb_=kb_: h.tensor_scalar(
                                out=vp[:, hh * 128:(hh + 1) * 128], in0=VA[:, kb_, hh * 128:(hh + 1) * 128], scalar1=wt_[:, kb:kb + 1], scalar2=None, op0=ALU.mult),
                                reads=[VA, wt_], writes=([vp] if hh == 0 else []), pwrites=([vp] if hh == 1 else []))

                    def plan_post(i, qt, kb, nkb, c0, n, pss, pt, vp):
                        sc.op("act", lambda h: h.activation(out=pt[:, 0:512 + n], in_=pss[:, 0:512 + n], func=AF.Exp), reads=[pss], writes=[pt])
                        if kb >= 4 * qt:
                            for hh in range(2):
                                sc.op("pool", lambda h, hh=hh: h.tensor_tensor(out=pt[:, hh * 512:hh * 512 + 128], in0=pt[:, hh * 512:hh * 512 + 128], in1=tri[:], op=ALU.mult),
                                      reads=[pt, tri], writes=[pt])
                        po = OO[qt % 2]
                        for hh in range(2):
                            hs = slice((1 - hh) * 64, (2 - hh) * 64)
                            ls = slice(hh * 64, (hh + 1) * 64)
                            rv = Rv[(qt % 2) * 2 + hh]
                            first = (kb == 0 and hh == 0)
                            sc.op("pe", lambda h, hh=hh: h.matmul(po[:, hh * 512 + c0:(hh + 1) * 512], lhsT=vp[:, hh * 128:(hh + 1) * 128], rhs=pt[:, hh * 512:hh * 512 + n],
                                                                  start=(kb == 0), stop=(kb == nkb - 1)),
                                  reads=[vp, pt], writes=([po] if first else []), pwrites=([] if first else [po]))
                        if kb == nkb - 1:
                            for hh in range(2):
                                hs = slice((1 - hh) * 64, (2 - hh) * 64)
                                ls = slice(hh * 64, (hh + 1) * 64)
                                rv = Rv[(qt % 2) * 2 + hh]
                                sc.op("dve", lambda h, rv=rv, ls=ls, hh=hh: h.reciprocal(out=rv[ls, :], in_=po[ls, hh * 512:(hh + 1) * 512]), reads=[po], writes=[rv])
                                sc.op("dve", lambda h, rv=rv, ls=ls, hs=hs, hh=hh: h.tensor_tensor(out=OT[hs, qt * 512:(qt + 1) * 512], in0=po[hs, hh * 512:(hh + 1) * 512],
                                                                                                    in1=rv[ls, :], op=ALU.mult),
                                      reads=[po, rv], pwrites=[OT])
                        if kb == nkb - 1 and qt % 4 == 3:
                            qq = qt // 4
                            for hh in range(2):
                                sc.dma("sp", lambda h, hh=hh: h.dma_start(out=o_in[qq * 128 + hh * 64:qq * 128 + (hh + 1) * 64, :],
                                                                          in_=OT[(1 - hh) * 64:(2 - hh) * 64, qq * TOK:(qq + 1) * TOK]), OT,
                                       reads=[OT], pwrites=[o_in])
                            sc.dma("pool", lambda h: h.collective_compute("AllGather", ALU.bypass, replica_groups=GROUPS,
                                                                          ins=[o_in.h[qq * 128:(qq + 1) * 128, :].rearrange("p (a b) -> (p a) b", b=512)],
                                                                          outs=[o_ag.h[qq * 512:(qq + 1) * 512, :].rearrange("p (a b) -> (p a) b", b=512)]),
                                   o_ag, reads=[o_in], pwrites=[o_ag], inc=1)

                    LAV = 8
                    for i in range(-LAV, len(steps)):
                        if 0 <= i + LAV < len(steps):
                            plan_vp(i + LAV)
                        if 0 <= i + LA < len(steps):
                            plan_qk(i + LA)
                        if i >= 0:
                            plan_rest(i)
                    dump(o_in, [512, TOK], BF16)
                    sc.emit(bsa, bsb, keep=[o_ag])
                if stop_phase == "C":
                    break
                with contextlib.ExitStack() as st:
                    PS, PSTs = alloc_psum(st, 6, 2)
                    OTo = sb(st, "OTo", [128, 4, TOK], BF16)
                    mT = sb(st, "mT", [128, 8, TOK], BF16)
                    sga = [sb(st, "sga%d" % i, [128, 512], BF16) for i in range(4)]
                    sgb = [sb(st, "sgb%d" % i, [128, 512], BF16) for i in range(4)]
                    m1 = [sb(st, "m1%d" % i, [128, 512], F32) for i in range(2)]
                    m2 = [sb(st, "m2%d" % i, [128, 512], F32) for i in range(2)]
                    xin = [sb(st, "xin%d" % i, [128, D], F32) for i in range(2)]
                    x1 = [sb(st, "x1%d" % i, [128, D], F32) for i in range(3)]
                    hb = [sb(st, "hb%d" % i, [128, D], BF16) for i in range(2)]
                    junk = sb(st, "junk", [128, D], BF16)
                    gB = sb(st, "gB", [128, D], F32)
                    ssq = sb(st, "ssq", [128, NT], F32)
                    lnv = sb(st, "lnv", [128, NT], F32)
                    rstd = sb(st, "rstd", [128, NT], F32)
                    sc.dma("sp", lambda h: h.dma_start(out=gB[:], in_=g_ffn[l:l + 1, :].partition_broadcast(128)), gB, reads=[g_ffn], writes=[gB])

                    def ldo(h):
                        v = dynv(h, "pool", "oq")
                        src = o_ag.h[bass.ds(v, 512), :].rearrange("(c p) t -> p c t", p=128)
                        return h.dma_start(out=OTo[:], in_=src)
                    sc.dma("pool", ldo, OTo, reads=[o_ag], writes=[OTo])
                    it = 0
                    for tg in range(4):
                        for fc in range(8):
                            pc = PS[it % 6]
                            a_ = sga[it % 4]
                            it += 1
                            sc.dma("sp", lambda h, a_=a_, fc=fc, tg=tg: h.dma_start(out=a_[:], in_=sg_sc[fc * 128:(fc + 1) * 128, tg * 512:(tg + 1) * 512]),
                                   a_, reads=[sg_sc], writes=[a_])
                            for i in range(4):
                                sc.op("pe", lambda h, i=i, pc=pc, fc=fc, tg=tg: h.matmul(pc[:], lhsT=wc[:, i, fc * 128:(fc + 1) * 128], rhs=cvT[:, i, tg * 512:(tg + 1) * 512],
                                                                                         start=(i == 0), stop=(i == 3)),
                                      reads=[wc, cvT], writes=([pc] if i == 0 else []), pwrites=([pc] if i > 0 else []))
                            sc.op("dve", lambda h, pc=pc, a_=a_, fc=fc, tg=tg: h.tensor_tensor(out=mT[:, fc, tg * 512:(tg + 1) * 512], in0=pc[:], in1=a_[:], op=ALU.mult),
                                  reads=[pc, a_], pwrites=[mT])
                    for tg in range(4):
                        for fc in range(8):
                            pa = PS[it % 6]
                            b_, m2_ = sgb[it % 4], m2[it % 2]
                            it += 1
                            sc.dma("sp", lambda h, b_=b_, fc=fc, tg=tg: h.dma_start(out=b_[:], in_=sg_sc[1024 + fc * 128:1024 + (fc + 1) * 128, tg * 512:(tg + 1) * 512]),
                                   b_, reads=[sg_sc], writes=[b_])
                            for i in range(4):
                                sc.op("pe", lambda h, i=i, pa=pa, fc=fc, tg=tg: h.matmul(pa[:], lhsT=wa[:, i, fc * 128:(fc + 1) * 128], rhs=OTo[:, i, tg * 512:(tg + 1) * 512],
                                                                                         start=(i == 0), stop=(i == 3)),
                                      reads=[wa, OTo], writes=([pa] if i == 0 else []), pwrites=([pa] if i > 0 else []))
                            sc.op("dve", lambda h, pa=pa, b_=b_, m2_=m2_: h.tensor_tensor(out=m2_[:], in0=pa[:], in1=b_[:], op=ALU.mult), reads=[pa, b_], writes=[m2_])
                            sc.op("pool" if it % 3 != 0 else "dve", lambda h, m2_=m2_, fc=fc, tg=tg: h.tensor_tensor(out=mT[:, fc, tg * 512:(tg + 1) * 512], in0=mT[:, fc, tg * 512:(tg + 1) * 512], in1=m2_[:], op=ALU.add),
                                  reads=[m2_, mT], pwrites=[mT])
                    sc.dma("sp", lambda h: h.dma_start(out=xin[0][:], in_=xcur[0:128, :]), xin[0], reads=[xcur], writes=[xin[0]])
                    for t in range(NT):
                        if t + 1 < NT:
                            sc.dma("sp", lambda h, t=t: h.dma_start(out=xin[(t + 1) % 2][:], in_=xcur[(t + 1) * 128:(t + 2) * 128, :]),
                                   xin[(t + 1) % 2], reads=[xcur], writes=[xin[(t + 1) % 2]])
                        xt, x1t = xin[t % 2], x1[t % 3]
                        for cg in range(2):
                            pp = PS[(2 * t + cg) % 4]
                            for c in range(8):
                                sc.op("pe", lambda h, c=c, pp=pp, t=t, cg=cg: h.matmul(pp[:], lhsT=mT[:, c, t * 128:(t + 1) * 128], rhs=wo[:, c, cg * 512:(cg + 1) * 512],
                                                                                       start=(c == 0), stop=(c == 7)),
                                      reads=[mT, wo], writes=([pp] if c == 0 else []), pwrites=([pp] if c > 0 else []))
                            sc.op("dve", lambda h, pp=pp, xt=xt, x1t=x1t, cg=cg: h.tensor_tensor(out=x1t[:, cg * 512:(cg + 1) * 512], in0=pp[:], in1=xt[:, cg * 512:(cg + 1) * 512], op=ALU.add),
                                  reads=[pp, xt], writes=([x1t] if cg == 0 else []), pwrites=([x1t] if cg == 1 else []))
                        sc.dma("sp", lambda h, x1t=x1t, t=t: h.dma_start(out=x1_sc[t * 128:(t + 1) * 128, :], in_=x1t[:]), x1t, reads=[x1t], pwrites=[x1_sc])
                        if t >= 1:
                            norm_transpose(st, t - 1, x1[(t - 1) % 3], gB, ssq, lnv, rstd, junk, hb[(t - 1) % 2])
                    norm_transpose(st, NT - 1, x1[(NT - 1) % 3], gB, ssq, lnv, rstd, junk, hb[(NT - 1) % 2])
                    dump(x1_sc, [TOK, D], F32)
                    sc.emit(bsa, bsb)
            if stop_phase in ("A00", "A01", "A0", "A", "A1", "B", "C", "D"):
                break
            with contextlib.ExitStack() as st:
                PS, PSTs = alloc_psum(st, 6, 0)
                w2 = sb(st, "w2", [128, NFC, D], BF16)
                actT = sb(st, "actT", [128, NFC, 1024], BF16)
                wg = [sb(st, "wg%d" % i, [128, 8, 256], BF16) for i in range(2)]
                wu = [sb(st, "wu%d" % i, [128, 8, 256], BF16) for i in range(2)]
                sl = [sb(st, "sl%d" % i, [128, 512], F32) for i in range(2)]
                xin = [sb(st, "xin%d" % i, [128, D], F32) for i in range(2)]
                xo = [sb(st, "xo%d" % i, [128, D], F32) for i in range(2)]

                def issue_f1(half, g):
                    a, b = wg[(half * 11 + g) % 2], wu[(half * 11 + g) % 2]
                    sc.dma("pool", lambda h: h.dma_start(out=a[:], in_=w_f1[l][:, g * 256:(g + 1) * 256].rearrange("(c p) n -> p c n", p=128)), a, reads=[w_f1], writes=[a])
                    sc.dma("pool", lambda h: h.dma_start(out=b[:], in_=w_f1[l][:, DFF + g * 256:DFF + (g + 1) * 256].rearrange("(c p) n -> p c n", p=128)), b, reads=[w_f1], writes=[b])

                issue_f1(0, 0)

                def issue_w2(fc):
                    sc.dma("pool", lambda h: h.dma_start(out=w2[:, fc, :], in_=w_f2[l][fc * 128:(fc + 1) * 128, :]), w2, reads=[w_f2], pwrites=[w2])
                it = 0
                for half in range(2):
                    for g in range(11):
                        if g + 1 < 11:
                            issue_f1(half, g + 1)
                        elif half == 0:
                            issue_f1(1, 0)
                        if half == 0:
                            issue_w2(2 * g)
                            issue_w2(2 * g + 1)
                        a, b = wg[(half * 11 + g) % 2], wu[(half * 11 + g) % 2]
                        for f2 in range(2):
                            fc = g * 2 + f2
                            for tg in range(2):
                                tok0 = half * 1024 + tg * 512
                                pg, pu = PS[(2 * it) % 6], PS[(2 * it + 1) % 6]
                                s_ = sl[it % 2]
                                it += 1
                                for c in range(8):
                                    sc.op("pe", lambda h, c=c, pg=pg, a=a, f2=f2, tok0=tok0: h.matmul(pg[:], lhsT=a[:, c, f2 * 128:(f2 + 1) * 128], rhs=hTs[tok0 // 512][:, c, :],
                                                                                                      start=(c == 0), stop=(c == 7)),
                                          reads=[a, hTs[tok0 // 512]], writes=([pg] if c == 0 else []), pwrites=([pg] if c > 0 else []))
                                for c in range(8):
                                    sc.op("pe", lambda h, c=c, pu=pu, b=b, f2=f2, tok0=tok0: h.matmul(pu[:], lhsT=b[:, c, f2 * 128:(f2 + 1) * 128], rhs=hTs[tok0 // 512][:, c, :],
                                                                                                      start=(c == 0), stop=(c == 7)),
                                          reads=[b, hTs[tok0 // 512]], writes=([pu] if c == 0 else []), pwrites=([pu] if c > 0 else []))
                                sc.op("act", lambda h, pg=pg, s_=s_: h.activation(out=s_[:], in_=pg[:], func=AF.Silu), reads=[pg], writes=[s_])
                                sc.op("dve", lambda h, pu=pu, s_=s_, fc=fc, tg=tg: h.tensor_tensor(out=actT[:, fc, tg * 512:(tg + 1) * 512], in0=pu[:], in1=s_[:], op=ALU.mult),
                                      reads=[pu, s_], pwrites=[actT])
                    for tt in range(8):
                        t = half * 8 + tt
                        xt, xot = xin[t % 2], xo[t % 2]
                        sc.dma("sp", lambda h, xt=xt, t=t: h.dma_start(out=xt[:], in_=x1_sc[t * 128:(t + 1) * 128, :]), xt, reads=[x1_sc], writes=[xt])
                        for cg in range(2):
                            pp = PS[(2 * t + cg) % 4]
                            for fc in range(NFC):
                                sc.op("pe", lambda h, fc=fc, pp=pp, tt=tt, cg=cg: h.matmul(pp[:], lhsT=actT[:, fc, tt * 128:(tt + 1) * 128], rhs=w2[:, fc, cg * 512:(cg + 1) * 512],
                                                                                           start=(fc == 0), stop=(fc == NFC - 1)),
                                      reads=[actT, w2], writes=([pp] if fc == 0 else []), pwrites=([pp] if fc > 0 else []))
                            sc.op("dve", lambda h, pp=pp, xt=xt, xot=xot, cg=cg: h.tensor_tensor(out=xot[:, cg * 512:(cg + 1) * 512], in0=pp[:], in1=xt[:, cg * 512:(cg + 1) * 512], op=ALU.add),
                                  reads=[pp, xt], writes=([xot] if cg == 0 else []), pwrites=([xot] if cg == 1 else []))
                        sc.dma("sp", lambda h, xot=xot, t=t: h.dma_start(out=xnext[t * 128:(t + 1) * 128, :], in_=xot[:]), xot, reads=[xot], pwrites=[xnext])
                sc.emit(bsa, bsb)
    return nc


_CACHE = {}


def _consts():
    c = np.zeros((128, 256), np.float32)
    c[:, 0:128] = np.eye(128, dtype=np.float32)
    c[:, 128:256] = np.triu(np.ones((128, 128), np.float32))
    return c


def make_in_maps(inputs):
    x = np.ascontiguousarray(inputs["x"], dtype=np.float32)
    consts = _consts()
    maps = []
    for c in range(8):
        b, j = c // 4, c % 4
        m = {k: np.ascontiguousarray(v, dtype=np.float32) for k, v in inputs.items() if k != "x"}
        m["x"] = np.ascontiguousarray(x[b, j * TOK:(j + 1) * TOK, :])
        m["consts"] = consts
        m["cmask"] = np.full((128, 1), 0.0 if j == 0 else 1.0, np.float32)
        maps.append(m)
    return maps


def kernel(**inputs):
    if "nc" not in _CACHE:
        _CACHE["nc"] = build(2, False)
    nc = _CACHE["nc"]
    maps = make_in_maps(inputs)
    res = run_bass_kernel_spmd(nc, maps, core_ids=list(range(8)))
    out = np.zeros((2, S, D), np.float32)
    for c in range(8):
        b, j = c // 4, c % 4
        out[b, j * TOK:(j + 1) * TOK, :] = res.results[c]["y"]
    return out
```
